# Optimizing a Trainium2 kernel written in Bass

```python
import math
import jax
import jax.numpy as jnp
from jax import lax
import numpy as np

D_MODEL = 2048
BATCH = 8
SEQ = 4096
DEPTH = 4

N_MIXERS = 4
GROUP_WIDTH = D_MODEL // N_MIXERS
HEAD_DIM = 128
N_HEADS = GROUP_WIDTH // HEAD_DIM
MIX_WIDTH = N_MIXERS * GROUP_WIDTH
Q_LORA = 512
KV_LORA = 512
QK_NOPE = 128
QK_ROPE = 64
V_HEAD = HEAD_DIM
MLA_QK_DIM = QK_NOPE + QK_ROPE
DILATED_PAIRS = ((128, 1), (512, 4), (2048, 16))
FORGET_BIAS_INIT = 2.0
BLOCK = 128
ROPE_THETA = 10000.0
FFN_HIDDEN = -(-8 * D_MODEL // (3 * 256)) * 256
EPS = 1e-6
NEG_INF = -1e30
IN_SPLITS = ((Q_LORA, KV_LORA, QK_ROPE)
             + (GROUP_WIDTH,) * 3
             + (GROUP_WIDTH,) * 3 + (N_HEADS,)
             + (GROUP_WIDTH,) * 3)
IN_WIDTH = sum(IN_SPLITS)

kernel_name = "hybrid_parallel_heads_mla_dilated_fox_stickbreak"


def rms_norm(x, gain):
    xf = x.astype(jnp.float32)
    y = xf * lax.rsqrt(jnp.mean(jnp.square(xf), axis=-1, keepdims=True) + EPS)
    return (y * gain.astype(jnp.float32)).astype(x.dtype)


def rope_tables(seq, dim):
    pos = jnp.arange(seq, dtype=jnp.float32)
    inv_freq = ROPE_THETA ** (-jnp.arange(0, dim, 2, dtype=jnp.float32) / dim)
    ang = pos[:, None] * inv_freq[None, :]
    return jnp.cos(ang), jnp.sin(ang)


def apply_rope(x, cos, sin):
    xf = x.astype(jnp.float32)
    x1, x2 = jnp.split(xf, 2, axis=-1)
    c, s = cos[:, None, :], sin[:, None, :]
    return jnp.concatenate([x1 * c - x2 * s, x1 * s + x2 * c], axis=-1).astype(x.dtype)


def split_heads(t):
    b, s, _ = t.shape
    return t.reshape(b, s, N_HEADS, -1)


def to_query_blocks(t):
    b, h, s = t.shape[:3]
    t = t.reshape((b, h, s // BLOCK, BLOCK) + t.shape[3:])
    return jnp.moveaxis(t, 2, 0)


def from_query_blocks(o):
    nb, b, h, _, d = o.shape
    return jnp.moveaxis(o, 0, 2).reshape(b, h, nb * BLOCK, d).transpose(0, 2, 1, 3)


def causal_softmax_attention(q, k, v, scale, cum_log_forget=None):
    s_len = q.shape[1]
    qh, kh, vh = (t.transpose(0, 2, 1, 3) for t in (q, k, v))
    key_pos = jnp.arange(s_len)
    block_idx = jnp.arange(s_len // BLOCK)
    if cum_log_forget is None:
        xs = (to_query_blocks(qh), block_idx)
    else:
        xs = (to_query_blocks(qh), block_idx, to_query_blocks(cum_log_forget))

    def body(args):
        qi, i = args[0], args[1]
        s = jnp.einsum('bhqd,bhkd->bhqk', qi, kh).astype(jnp.float32) * scale
        if cum_log_forget is not None:
            s = s + args[2][..., None] - cum_log_forget[:, :, None, :]
        q_pos = i * BLOCK + jnp.arange(BLOCK)
        s = jnp.where(key_pos[None, :] <= q_pos[:, None], s, NEG_INF)
        p = jax.nn.softmax(s, axis=-1).astype(vh.dtype)
        return jnp.einsum('bhqk,bhkd->bhqd', p, vh)

    return from_query_blocks(lax.map(body, xs))


def mla_attention(q_lat, kv_lat, k_rope, q_norm, w_uq, kv_norm, w_ukv, cos, sin):
    b, s, _ = q_lat.shape
    q = (rms_norm(q_lat, q_norm) @ w_uq).reshape(b, s, N_HEADS, MLA_QK_DIM)
    q = jnp.concatenate([q[..., :QK_NOPE], apply_rope(q[..., QK_NOPE:], cos, sin)], axis=-1)
    kv = (rms_norm(kv_lat, kv_norm) @ w_ukv).reshape(b, s, N_HEADS, QK_NOPE + V_HEAD)
    k_nope, v = kv[..., :QK_NOPE], kv[..., QK_NOPE:]
    k_pe = apply_rope(k_rope[:, :, None, :], cos, sin)
    k = jnp.concatenate([k_nope, jnp.broadcast_to(k_pe, (b, s, N_HEADS, QK_ROPE))], axis=-1)
    return causal_softmax_attention(q, k, v, MLA_QK_DIM ** -0.5)


def banded_window_attention(q, k, v, steps, scale):
    n, h, l, d = q.shape
    pad = (-l) % BLOCK
    cfg = ((0, 0), (0, 0), (0, pad), (0, 0))
    q, k, v = jnp.pad(q, cfg), jnp.pad(k, cfg), jnp.pad(v, cfg)
    nb = (l + pad) // BLOCK
    qb = q.reshape(n, h, nb, BLOCK, d)

    def with_prev(t):
        tb = t.reshape(n, h, nb, BLOCK, d)
        prev = jnp.concatenate([jnp.zeros_like(tb[:, :, :1]), tb[:, :, :-1]], axis=2)
        return jnp.concatenate([prev, tb], axis=3)

    kb, vb = with_prev(k), with_prev(v)
    s = jnp.einsum('nhbqd,nhbkd->nhbqk', qb, kb).astype(jnp.float32) * scale
    q_idx = jnp.arange(BLOCK)
    k_idx = jnp.arange(2 * BLOCK)
    dist = BLOCK + q_idx[:, None] - k_idx[None, :]
    key_pos = (jnp.arange(nb)[:, None] - 1) * BLOCK + k_idx[None, :]
    valid = ((dist >= 0) & (dist <= steps))[None] & (key_pos >= 0)[:, None, :]
    s = jnp.where(valid, s, NEG_INF)
    lse = jax.nn.logsumexp(s, axis=-1)
    p = jnp.exp(s - lse[..., None]).astype(v.dtype)
    out = jnp.einsum('nhbqk,nhbkd->nhbqd', p, vb).reshape(n, h, nb * BLOCK, d)[:, :, :l]
    return out, lse.reshape(n, h, nb * BLOCK)[:, :, :l]


def dilated_window_attention(q, k, v):
    b, s, h, d = q.shape
    scale = d ** -0.5
    outs, lses = [], []
    for window, dilation in DILATED_PAIRS:
        l = s // dilation

        def by_residue(t):
            return t.reshape(b, l, dilation, h, d).transpose(0, 2, 3, 1, 4).reshape(b * dilation, h, l, d)

        o, lse = banded_window_attention(by_residue(q), by_residue(k), by_residue(v),
                                         window // dilation, scale)
        outs.append(o.reshape(b, dilation, h, l, d).transpose(0, 3, 1, 2, 4).reshape(b, s, h, d))
        lses.append(lse.reshape(b, dilation, h, l).transpose(0, 3, 1, 2).reshape(b, s, h))
    weights = jax.nn.softmax(jnp.stack(lses), axis=0).astype(q.dtype)
    return jnp.einsum('gbsh,gbshd->bshd', weights, jnp.stack(outs))


def forgetting_attention(q, k, v, f_logit, f_bias):
    log_f = jax.nn.log_sigmoid((f_logit + f_bias).astype(jnp.float32))
    cum = jnp.cumsum(log_f, axis=1).transpose(0, 2, 1)
    return causal_softmax_attention(q, k, v, HEAD_DIM ** -0.5, cum)


def stick_breaking_attention(q, k, v):
    s_len, d = q.shape[1], q.shape[-1]
    scale = d ** -0.5
    qh, kh, vh = (t.transpose(0, 2, 1, 3) for t in (q, k, v))
    key_pos = jnp.arange(s_len)

    def body(args):
        qi, i = args
        z = jnp.einsum('bhqd,bhkd->bhqk', qi, kh).astype(jnp.float32) * scale
        q_pos = i * BLOCK + jnp.arange(BLOCK)
        past = key_pos[None, :] < q_pos[:, None]
        log_keep = jnp.where(past, jax.nn.log_sigmoid(-z), 0.0)
        between = lax.cumsum(log_keep, axis=3, reverse=True) - log_keep
        a = jnp.where(past, jnp.exp(jax.nn.log_sigmoid(z) + between), 0.0)
        return jnp.einsum('bhqk,bhkd->bhqd', a.astype(vh.dtype), vh)

    xs = (to_query_blocks(qh), jnp.arange(s_len // BLOCK))
    return from_query_blocks(lax.map(body, xs))


def hybrid_mixer(h, w_in, mla_q_norm, w_uq, mla_kv_norm, w_ukv, fox_forget_bias,
                 group_norm, w_out, rope_full, rope_mla):
    b, s, _ = h.shape
    proj = h @ w_in
    offsets = np.cumsum(IN_SPLITS)[:-1].tolist()
    (q_lat, kv_lat, k_rope, q_b, k_b, v_b, q_c, k_c, v_c, f_c, q_d, k_d, v_d) = \
        jnp.split(proj, offsets, axis=-1)
    cos, sin = rope_full
    out_a = mla_attention(q_lat, kv_lat, k_rope, mla_q_norm, w_uq, mla_kv_norm, w_ukv, *rope_mla)
    out_b = dilated_window_attention(apply_rope(split_heads(q_b), cos, sin),
                                     apply_rope(split_heads(k_b), cos, sin), split_heads(v_b))
    out_c = forgetting_attention(split_heads(q_c), split_heads(k_c), split_heads(v_c),
                                 f_c, fox_forget_bias)
    out_d = stick_breaking_attention(split_heads(q_d), split_heads(k_d), split_heads(v_d))
    groups = jnp.stack([o.reshape(b, s, GROUP_WIDTH) for o in (out_a, out_b, out_c, out_d)],
                       axis=2)
    groups = rms_norm(groups, group_norm.reshape(N_MIXERS, GROUP_WIDTH))
    return groups.reshape(b, s, MIX_WIDTH) @ w_out


def setup_inputs(seed: int = 0) -> dict:
    key = jax.random.key(seed)
    ks = jax.random.split(key, 16)

    def w(k, shape, fan_in):
        return jax.random.normal(k, shape, jnp.float32) * fan_in ** -0.5

    def gain(k, shape):
        return 1.0 + 0.05 * jax.random.normal(k, shape, jnp.float32)

    return {
        "x": jax.random.normal(ks[0], (BATCH, SEQ, D_MODEL), jnp.float32),
        "attn_norm": gain(ks[1], (DEPTH, D_MODEL)),
        "w_in": w(ks[2], (DEPTH, D_MODEL, IN_WIDTH), D_MODEL),
        "mla_q_norm": gain(ks[3], (DEPTH, Q_LORA)),
        "w_uq": w(ks[4], (DEPTH, Q_LORA, N_HEADS * MLA_QK_DIM), Q_LORA),
        "mla_kv_norm": gain(ks[5], (DEPTH, KV_LORA)),
        "w_ukv": w(ks[6], (DEPTH, KV_LORA, N_HEADS * (QK_NOPE + V_HEAD)), KV_LORA),
        "fox_forget_bias": FORGET_BIAS_INIT + 0.1 * jax.random.normal(ks[7], (DEPTH, N_HEADS), jnp.float32),
        "group_norm": gain(ks[8], (DEPTH, MIX_WIDTH)),
        "w_out": w(ks[9], (DEPTH, MIX_WIDTH, D_MODEL), MIX_WIDTH),
        "ffn_norm": gain(ks[10], (DEPTH, D_MODEL)),
        "w_gate": w(ks[11], (DEPTH, D_MODEL, FFN_HIDDEN), D_MODEL),
        "w_up": w(ks[12], (DEPTH, D_MODEL, FFN_HIDDEN), D_MODEL),
        "w_down": w(ks[13], (DEPTH, FFN_HIDDEN, D_MODEL), FFN_HIDDEN),
        "final_norm": gain(ks[14], (D_MODEL,)),
    }


def reference(x, attn_norm, w_in, mla_q_norm, w_uq, mla_kv_norm, w_ukv, fox_forget_bias,
              group_norm, w_out, ffn_norm, w_gate, w_up, w_down, final_norm):
    s_len = x.shape[1]
    rope_full = rope_tables(s_len, HEAD_DIM)
    rope_mla = rope_tables(s_len, QK_ROPE)
    for l in range(DEPTH):
        x = x + hybrid_mixer(rms_norm(x, attn_norm[l]), w_in[l], mla_q_norm[l], w_uq[l],
                             mla_kv_norm[l], w_ukv[l], fox_forget_bias[l], group_norm[l],
                             w_out[l], rope_full, rope_mla)
        h = rms_norm(x, ffn_norm[l])
        x = x + (jax.nn.silu(h @ w_gate[l]) * (h @ w_up[l])) @ w_down[l]
    return rms_norm(x, final_norm)
```

```python
import math
from contextlib import ExitStack
import numpy as np
import ml_dtypes
import concourse.bass as bass
import concourse.mybir as mybir
from concourse.bass_utils import run_bass_kernel_spmd

F32 = mybir.dt.float32
BF16 = mybir.dt.bfloat16
AF = mybir.ActivationFunctionType
ALU = mybir.AluOpType

D = 2048
KC = 16
FF = 5632
INW = 5700
NEG = -30000.0
EPS = 1e-6
NDS = 24


class Buf:
    __slots__ = ("name", "w", "r")

    def __init__(self, name):
        self.name = name
        self.w = None
        self.r = {}


class _Rec:
    def __getattr__(self, name):
        return lambda *a, **k: (name, a, k)


_REC = _Rec()


def alias(new_bufs, old_bufs):
    toks = {}
    for b in old_bufs:
        for t in ([b.w] if b.w is not None else []) + list(b.r.values()):
            if toks.get(t[0], 0) < t[1]:
                toks[t[0]] = t[1]
    for nb in new_bufs:
        nb.w = None
        nb.r = {("al", i): (k, v) for i, (k, v) in enumerate(toks.items())}


class Prog:
    def __init__(self, nc, es):
        self.nc = nc
        self.streams = {k: [] for k in ("pe", "act", "dve", "pool", "sp")}
        self.count = {k: 0 for k in self.streams}
        self.sem = {k: es.enter_context(nc.semaphore("s_" + k)) for k in ("pe", "act", "dve", "pool")}
        self.dsem = [es.enter_context(nc.semaphore("d%d" % i)) for i in range(NDS)]
        self.dcount = 0
        self.waited = {k: {} for k in self.streams}

    def op(self, eng, fn, reads=(), writes=()):
        toks = []
        for b in reads:
            if b.w is not None:
                toks.append(b.w)
        for b in writes:
            if b.w is not None:
                toks.append(b.w)
            toks.extend(b.r.values())
        if eng == "sp":
            j = self.dcount % NDS
            n = self.dcount // NDS + 1
            self.dcount += 1
            key = ("d", j)
            tok = (key, 16 * n)
            if n > 1:
                toks.append((key, 16 * (n - 1)))
            rkey = key
        else:
            self.count[eng] += 1
            tok = (eng, self.count[eng])
            rkey = eng
        need = {}
        wd = self.waited[eng]
        for (k, v) in toks:
            if k == "pe" and eng == "pe":
                continue
            if wd.get(k, 0) >= v:
                continue
            if need.get(k, 0) < v:
                need[k] = v
        for k, v in need.items():
            wd[k] = v
        self.streams[eng].append((list(need.items()), fn(_REC), tok))
        for b in reads:
            b.r[rkey] = tok
        for b in writes:
            b.w = tok
            b.r = {}
        return tok

    def _sem(self, key):
        return self.dsem[key[1]] if isinstance(key, tuple) else self.sem[key]

    def finish(self):
        waits = []
        for j in range(NDS):
            n = (self.dcount - j + NDS - 1) // NDS if self.dcount > j else 0
            if n > 0:
                waits.append((("d", j), 16 * n))
        self.streams["sp"].append((waits, None, None))

    def emit(self):
        nc = self.nc
        with nc.Block() as block:
            def run(e, key):
                for waits, fn, tok in self.streams[key]:
                    for (k, v) in waits:
                        e.wait_ge(self._sem(k), v)
                    if fn is None:
                        continue
                    inst = getattr(e, fn[0])(*fn[1], **fn[2])
                    inst.then_inc(self._sem(tok[0]), 16 if isinstance(tok[0], tuple) else 1)

            @block.tensor
            def _(e):
                run(e, "pe")

            @block.scalar
            def _(e):
                run(e, "act")

            @block.vector
            def _(e):
                run(e, "dve")

            @block.sync
            def _(e):
                run(e, "sp")


class Pool:
    def __init__(self, views, name):
        self.v = views
        self.b = [Buf("%s%d" % (name, i)) for i in range(len(views))]
        self.i = 0

    def get(self):
        i = self.i
        self.i = (i + 1) % len(self.v)
        return self.v[i], self.b[i]


def host_consts(S):
    bf = ml_dtypes.bfloat16
    kk = np.arange(128)[:, None]
    c = {}
    c["ident"] = np.eye(128, dtype=np.float32).astype(bf)
    c["ones"] = np.ones((128, 128), np.float32).astype(bf)
    c["negones"] = (-np.ones((128, 128), np.float32)).astype(bf)
    jj = np.arange(128)[:, None]
    k2 = np.arange(128)[None, :]
    c["negtri"] = np.where(jj >= k2, -1.0, 0.0).astype(np.float32).astype(bf)
    cc = np.arange(896)[None, :]
    dl = cc - kk - 384
    c["maskA"] = np.where(dl >= 0, 0.0, NEG).astype(np.float32).astype(bf)
    c["maskD"] = np.where(dl >= 1, 0.0, NEG).astype(np.float32).astype(bf)
    cc = np.arange(3072)[None, :]
    dl = cc - kk - 384
    mult = ((dl >= 0) & (dl <= 128)).astype(np.int32) + ((dl >= 0) & (dl % 4 == 0) & (dl <= 512)) \
        + ((dl >= 0) & (dl % 16 == 0) & (dl <= 2048))
    c["maskB"] = np.where(mult > 0, np.log(np.maximum(mult, 1).astype(np.float32)), NEG).astype(np.float32).astype(bf)
    pos = np.arange(S, dtype=np.float32)

    def tab(dim):
        inv = np.float32(10000.0) ** (-(np.arange(0, dim, 2, dtype=np.float32) / np.float32(dim)))
        ang = (pos[:, None] * inv[None, :]).astype(np.float32)
        co = np.cos(ang).astype(np.float32).T
        si = np.sin(ang).astype(np.float32).T
        return (np.ascontiguousarray(np.concatenate([co, co], 0)),
                np.ascontiguousarray(np.concatenate([si, si], 0)))

    c["cosB"], c["sinB"] = tab(128)
    c["cosM"], c["sinM"] = tab(64)
    c["onef"] = np.ones((1, 512), np.float32)
    return c


def build(S, DEPTH):
    NT = S // 128
    NQ = S // 512
    NC2 = S // 256
    nc = bass.Bass("TRN2", target_bir_lowering=False, dynamic_dma_scratch_size=256)
    es = ExitStack()

    def din(name, shape, dt=F32):
        return nc.dram_tensor(name, list(shape), dt, kind="ExternalInput").ap()

    def dscr(name, shape, dt):
        return nc.dram_tensor(name, list(shape), dt, kind="Internal").ap()

    xT = din("xT", [D, S])
    attn_norm = din("attn_norm", [DEPTH, D])
    w_in = din("w_in", [DEPTH, D, INW])
    mla_q_norm = din("mla_q_norm", [DEPTH, 512])
    w_uq = din("w_uq", [DEPTH, 512, 768])
    mla_kv_norm = din("mla_kv_norm", [DEPTH, 512])
    w_ukv = din("w_ukv", [DEPTH, 512, 1024])
    fox_b = din("fox_forget_bias", [DEPTH, 4])
    group_norm = din("group_norm", [DEPTH, D])
    w_out = din("w_out", [DEPTH, D, D])
    ffn_norm = din("ffn_norm", [DEPTH, D])
    w_gate = din("w_gate", [DEPTH, D, FF])
    w_up = din("w_up", [DEPTH, D, FF])
    w_down = din("w_down", [DEPTH, FF, D])
    final_norm = din("final_norm", [D])
    c_ident = din("ident", [128, 128], BF16)
    c_ones = din("ones", [128, 128], BF16)
    c_negones = din("negones", [128, 128], BF16)
    c_negtri = din("negtri", [128, 128], BF16)
    c_maskA = din("maskA", [128, 896], BF16)
    c_maskD = din("maskD", [128, 896], BF16)
    c_maskB = din("maskB", [128, 3072], BF16)
    c_cosB = din("cosB", [128, S])
    c_sinB = din("sinB", [128, S])
    c_cosM = din("cosM", [64, S])
    c_sinM = din("sinM", [64, S])
    c_onef = din("onef", [1, 512])
    yT = nc.dram_tensor("yT", [D, S], F32, kind="ExternalOutput").ap()

    latT = dscr("latT", [8, 128, S], F32)
    qT = dscr("qT", [16, 128, S], BF16)
    kT = dscr("kT", [16, 128, S], BF16)
    qrT = dscr("qrT", [4, 64, S], BF16)
    kpeT = dscr("kpeT", [64, S], BF16)
    vS = dscr("vS", [16, 128, NT * 128], BF16)
    cqS = dscr("cqS", [4, S], BF16)
    attnT = dscr("attnT", [16, 128, S], BF16)
    actT = dscr("actT", [44, 128, S], BF16)
    yTB = [Buf("yT%d" % i) for i in range(16)]
    latTB = Buf("latT")
    qTB = [Buf("qT%d" % i) for i in range(16)]
    kTB = [Buf("kT%d" % i) for i in range(16)]
    qrTB, kpeTB, cqSB = Buf("qrT"), Buf("kpeT"), Buf("cqS")
    vSB = [Buf("vS%d" % i) for i in range(16)]
    attnTB = [Buf("at%d" % i) for i in range(16)]
    actTB = [Buf("ac%d" % i) for i in range(44)]

    def sb(name, shape, dt):
        return es.enter_context(nc.sbuf_tensor("s_" + name, list(shape), dt))

    P = Prog(nc, es)
    A0 = sb("A0", [128, 65536], BF16)
    A1 = sb("A1", [128, 22528], BF16)
    PS = [es.enter_context(nc.psum_tensor("ps%d" % i, [128, 512], F32)) for i in range(8)]
    PSb = [Buf("ps%d" % i) for i in range(8)]

    def a0(off, n):
        return A0[:, off:off + n]

    def f32v(ap):
        return ap.bitcast(F32)

    ident = sb("ident", [128, 128], BF16)
    ones = sb("ones", [128, 128], BF16)
    negones = sb("negones", [128, 128], BF16)
    negtri = sb("negtri", [128, 128], BF16)
    maskA = sb("maskA", [128, 896], BF16)
    maskD = sb("maskD", [128, 896], BF16)
    maskB = sb("maskB", [128, 3072], BF16)
    onef = sb("onef", [1, 512], F32)
    gains = sb("gains", [128, 80], F32)
    foxb = sb("foxb", [1, 8], F32)
    cnegcol = sb("cnegcol", [128, 4 * NT], F32)
    cB, gB, cnB = Buf("consts"), Buf("gains"), Buf("cnegcol")

    def mkpool(name, n, shape, dt):
        return Pool([sb("%s%d" % (name, i), shape, dt)[:] for i in range(n)], name)

    pt_pool = mkpool("pt", 3, [128, 512], BF16)
    ev_pool = mkpool("ev", 3, [128, 512], BF16)
    xf_pool = mkpool("xf", 3, [128, 512], F32)
    tb_pool = mkpool("tb", 2, [128, 512], F32)
    tm_pool = mkpool("tm", 3, [128, 512], F32)
    ln_pool = mkpool("ln", 3, [128, 512], BF16)
    rr_pool = mkpool("rr", 3, [128, 512], BF16)
    row_pool = mkpool("row", 2, [1, 512], F32)
    cn_pool = mkpool("cn", 2, [1, 512], F32)
    rb_pool = mkpool("rb", 2, [1, 512], BF16)

    def dma(out, in_, reads=(), writes=()):
        P.op("sp", (lambda e: e.dma_start(out=out, in_=in_)), reads=reads, writes=writes)

    for (t, src) in ((ident, c_ident), (ones, c_ones), (negones, c_negones), (negtri, c_negtri),
                     (maskA, c_maskA), (maskD, c_maskD), (maskB, c_maskB), (onef, c_onef)):
        dma(t[:], src, writes=[cB])

    gemm_rot = [0]

    def gbank(n=4, base=0):
        i = base + gemm_rot[0] % n
        gemm_rot[0] += 1
        return i

    stage = f32v(A1[:, 0:11264])
    stageB = Buf("stage")
    wslots = [A1[:, 11264:16896], A1[:, 16896:22528]]
    wslotB = [Buf("w0"), Buf("w1")]
    wrot = [0]

    def load_stage(src3, kc, w):
        st = stage[:, 0:kc * w].rearrange("p (c n) -> p c n", c=kc)
        dma(st, src3, writes=[stageB])
        return st

    def wslot(kc, w):
        i = wrot[0] % 2
        wrot[0] += 1
        return wslots[i][:, 0:kc * w].rearrange("p (c n) -> p c n", c=kc), wslotB[i]

    def cast(out, in_, rb, wb, scale=None):
        if scale is None:
            P.op("act", (lambda e: e.activation(out=out, in_=in_, func=AF.Copy)), reads=rb, writes=wb)
        else:
            P.op("act", (lambda e: e.activation(out=out, in_=in_, func=AF.Copy, scale=scale)), reads=rb, writes=wb)

    stage_list = [(stage, stageB)]
    srot = [0]

    def wtile(src3, kc, w):
        sv, sB = stage_list[srot[0] % len(stage_list)]
        srot[0] += 1
        st = sv[:, 0:kc * w].rearrange("p (c n) -> p c n", c=kc)
        dma(st, src3, writes=[sB])
        wv, wb = wslot(kc, w)
        cast(wv, st, [sB], [wb])
        return wv, wb, st

    def wtile_rot(src3, kc, w, half):
        st = load_stage(src3, kc, w)
        wv, wb = wslot(kc, 2 * w)
        cast(wv[:, :, 0:w], st, [stageB], [wb])
        cast(wv[:, :, w:w + half], st[:, :, half:w], [stageB], [wb], -1.0)
        cast(wv[:, :, w + half:2 * w], st[:, :, 0:half], [stageB], [wb])
        return wv, wb

    def wtile2(srcA, srcB, kc, w):
        st = stage[:, 0:kc * 2 * w].rearrange("p (c n) -> p c n", c=kc)
        dma(st[:, :, 0:w], srcA, writes=[stageB])
        dma(st[:, :, w:2 * w], srcB, writes=[stageB])
        wv, wb = wslot(kc, 2 * w)
        cast(wv, st, [stageB], [wb])
        return wv, wb

    def pipelined(steps):
        if not steps:
            return
        nxt = steps[0][0]()
        for i, (_, comp) in enumerate(steps):
            cur = nxt
            if i + 1 < len(steps):
                nxt = steps[i + 1][0]()
            comp(cur)

    hT = a0(0, 16 * S).rearrange("p (c s) -> p c s", c=16)
    hTb = [Buf("hT%d" % i) for i in range(NC2)]
    a0_cur = [list(hTb)]

    def a0_switch(new_bufs):
        alias(new_bufs, a0_cur[0])
        a0_cur[0] = list(new_bufs)

    def hb(tc):
        return [hTb[2 * tc], hTb[2 * tc + 1]]

    xcs = [f32v(A1[:, i * 8192:(i + 1) * 8192]).rearrange("p (c s) -> p c s", c=16) for i in range(2)]
    sq = A1[:, 16384:20480].rearrange("p (c s) -> p c s", c=16)
    xcBs, sqB = [Buf("xc0"), Buf("xc1")], Buf("sqn")
    a1_perm = [stageB, wslotB[0], wslotB[1]]
    a1_norm = xcBs + [sqB]

    def rstd_from(bk, inv_n, width):
        sd, sdb = tm_pool.get()
        P.op("act", (lambda e: e.activation(out=sd[:, 0:width], in_=PS[bk][:, 0:width], func=AF.Ln,
                                            bias=EPS, scale=inv_n)), reads=[PSb[bk]], writes=[sdb])
        rs, rsb = tm_pool.get()
        P.op("act", (lambda e: e.activation(out=rs[:, 0:width], in_=sd[:, 0:width], func=AF.Exp, scale=-0.5)),
             reads=[sdb], writes=[rsb])
        return rs, rsb

    def norm_phase(src, dst_fn):
        srcv = src.rearrange("(c p) s -> p c s", p=128)
        alias(a1_norm, a1_perm)
        for i in range(NC2):
            sl = slice(i * 256, (i + 1) * 256)
            xc, xcB = xcs[i % 2], xcBs[i % 2]
            dma(xc, srcv[:, :, sl], reads=yTB, writes=[xcB])
            P.op("act", (lambda e: e.activation(out=sq, in_=xc, func=AF.Square)), reads=[xcB], writes=[sqB])
            bk = 7
            for c in range(16):
                P.op("pe", (lambda e: e.matmul(PS[bk][:, 0:256], ones[:], sq[:, c, :], start=(c == 0),
                                               stop=(c == 15))), reads=[sqB, cB], writes=[PSb[bk]])
            rs, rsb = rstd_from(bk, 1.0 / D, 256)
            dst_fn(i, sl, rs, rsb, xc, xcB)
        alias(a1_perm, a1_norm)

    def norm_to_hT(src, g0):
        def dst(i, sl, rs, rsb, xc, xcB):
            for c in range(16):
                P.op("dve", (lambda e: e.scalar_tensor_tensor(
                    out=hT[:, c, sl], in0=xc[:, c, :], scalar=gains[:, g0 + c:g0 + c + 1], in1=rs[:, 0:256],
                    op0=ALU.mult, op1=ALU.mult)), reads=[xcB, rsb, gB], writes=[hTb[i]])
        norm_phase(src, dst)

    def gemm_fm(wv, wb, kc, col0, m, actv, abf, evac, nbank=4, base=0):
        for tc0 in range(0, NQ, 2):
            tcs = [t_ for t_ in (tc0, tc0 + 1) if t_ < NQ]
            bks = [gbank(nbank, base) for _ in tcs]
            if hasattr(evac, "prefetch"):
                for tc in tcs:
                    evac.prefetch(tc, slice(tc * 512, (tc + 1) * 512))
            for c in range(kc):
                for bk, tc in zip(bks, tcs):
                    sl = slice(tc * 512, (tc + 1) * 512)
                    P.op("pe", (lambda e: e.matmul(PS[bk][0:m, :], wv[:, c, col0:col0 + m], actv[:, c, sl],
                                                   start=(c == 0), stop=(c == kc - 1))),
                         reads=[wb] + abf(tc), writes=[PSb[bk]])
            for bk, tc in zip(bks, tcs):
                evac(tc, slice(tc * 512, (tc + 1) * 512), bk)

    def store_bf(dst, dstB, scale=None, m=128):
        def evac(tc, sl, bk):
            o, ob = ev_pool.get()
            if scale is None:
                P.op("dve", (lambda e: e.tensor_copy(out=o[0:m, :], in_=PS[bk][0:m, :])), reads=[PSb[bk]], writes=[ob])
            else:
                P.op("dve", (lambda e: e.tensor_scalar(out=o[0:m, :], in0=PS[bk][0:m, :], scalar1=scale, scalar2=None,
                                                       op0=ALU.mult)), reads=[PSb[bk]], writes=[ob])
            dma(dst[0:m, sl], o[0:m, :], reads=[ob], writes=[dstB])
        return evac

    def rope_evac(b1, b2, m, sl, cosd, sind, dst, dstB, scale):
        ct, ctb = tb_pool.get()
        dma(ct[0:m, :], cosd[0:m, sl], writes=[ctb])
        st_, stb = tb_pool.get()
        dma(st_[0:m, :], sind[0:m, sl], writes=[stb])
        t1, t1b = tm_pool.get()
        P.op("dve", (lambda e: e.scalar_tensor_tensor(out=t1[0:m, :], in0=PS[b1][0:m, :], scalar=scale,
                                                      in1=ct[0:m, :], op0=ALU.mult, op1=ALU.mult)),
             reads=[PSb[b1], ctb], writes=[t1b])
        t2, t2b = tm_pool.get()
        P.op("dve", (lambda e: e.scalar_tensor_tensor(out=t2[0:m, :], in0=PS[b2][0:m, :], scalar=scale,
                                                      in1=st_[0:m, :], op0=ALU.mult, op1=ALU.mult)),
             reads=[PSb[b2], stb], writes=[t2b])
        o, ob = ev_pool.get()
        P.op("dve", (lambda e: e.tensor_tensor(out=o[0:m, :], in0=t1[0:m, :], in1=t2[0:m, :], op=ALU.add)),
             reads=[t1b, t2b], writes=[ob])
        dma(dst[0:m, sl], o[0:m, :], reads=[ob], writes=[dstB])

    def rope_fm(wv, wr, wbufs, kc, col0, m, actv, abf, cosd, sind, dst, dstB, scale):
        for tc in range(NQ):
            sl = slice(tc * 512, (tc + 1) * 512)
            b1 = gbank()
            b2 = gbank()
            for c in range(kc):
                for (bk, w_, wb_) in ((b1, wv, wbufs[0]), (b2, wr, wbufs[1])):
                    P.op("pe", (lambda e: e.matmul(PS[bk][0:m, :], w_[:, c, col0:col0 + m], actv[:, c, sl],
                                                   start=(c == 0), stop=(c == kc - 1))),
                         reads=[wb_] + abf(tc), writes=[PSb[bk]])
            rope_evac(b1, b2, m, sl, cosd, sind, dst, dstB, scale)

    SC128 = 128 ** -0.5
    SC192 = 192 ** -0.5

    def resid_load(fb, src, sl, buf=None):
        xo, xob = buf if buf is not None else xf_pool.get()
        dma(xo, src[fb * 128:(fb + 1) * 128, sl], reads=[yTB[fb]], writes=[xob])
        return xo, xob

    def resid_add_store(fb, sl, bk, xo, xob):
        P.op("dve", (lambda e: e.tensor_tensor(out=xo, in0=PS[bk][:], in1=xo, op=ALU.add)),
             reads=[PSb[bk], xob], writes=[xob])
        dma(yT[fb * 128:(fb + 1) * 128, sl], xo, reads=[xob], writes=[yTB[fb]])

    def resid_evac(fb, src):
        pre = {}

        def prefetch(tc, sl):
            pre[tc] = resid_load(fb, src, sl)

        def evac(tc, sl, bk):
            xo, xob = pre.pop(tc) if tc in pre else resid_load(fb, src, sl)
            resid_add_store(fb, sl, bk, xo, xob)
        evac.prefetch = prefetch
        return evac

    for l in range(DEPTH):
        xsrc = xT if l == 0 else yT

        def gl(dst_c0, vec, n):
            dma(gains[:, dst_c0:dst_c0 + n], vec.rearrange("(c p) -> p c", p=128), writes=[gB])
        gl(0, attn_norm[l], 16)
        gl(16, ffn_norm[l], 16)
        gl(32, group_norm[l], 16)
        gl(48, mla_q_norm[l], 4)
        gl(52, mla_kv_norm[l], 4)
        if l == 0:
            gl(56, final_norm, 16)
        dma(foxb[0:1, 0:4], fox_b[l:l + 1, :], writes=[gB])
        P.op("dve", (lambda e: e.tensor_scalar(out=foxb[0:1, 4:8], in0=foxb[0:1, 0:4], scalar1=-1.0, scalar2=None,
                                               op0=ALU.mult)), reads=[gB], writes=[gB])

        if l > 0:
            a0_switch(hTb)
        norm_to_hT(xsrc, 0)

        win = w_in[l].rearrange("(c p) n -> p c n", p=128)
        steps = []

        def lat_evac(fb):
            def evac(tc, sl, bk):
                o, ob = xf_pool.get()
                P.op("dve", (lambda e: e.tensor_copy(out=o, in_=PS[bk][:])), reads=[PSb[bk]], writes=[ob])
                dma(latT[fb][:, sl], o, reads=[ob], writes=[latTB])
            return evac

        for t in range(4):
            def ld(t=t):
                wv, wb, _ = wtile(win[:, :, t * 256:(t + 1) * 256], 16, 256)
                return wv, wb

            def cp(hd, t=t):
                wv, wb = hd
                for b in range(2):
                    gemm_fm(wv, wb, 16, b * 128, 128, hT, hb, lat_evac(t * 2 + b))
            steps.append((ld, cp))

        def ld_kr():
            return wtile_rot(win[:, :, 1024:1088], 16, 64, 32)

        def cp_kr(hd):
            wv, wb = hd
            rope_fm(wv[:, :, 0:64], wv[:, :, 64:128], (wb, wb), 16, 0, 64, hT, hb, c_cosM, c_sinM, kpeT, kpeTB, 1.0)
        steps.append((ld_kr, cp_kr))

        def v_gemm(wv, wb, ncols, hh0, nh):
            vb_ = {}
            for t0 in range(0, NT, 2):
                vb_[t0], vb_[t0 + 1] = gbank(), gbank()
                for c in range(16):
                    for t in (t0, t0 + 1):
                        bk = vb_[t]
                        ts = slice(t * 128, (t + 1) * 128)
                        P.op("pe", (lambda e: e.matmul(PS[bk][:, 0:ncols], hT[:, c, ts], wv[:, c, 0:ncols],
                                                       start=(c == 0), stop=(c == 15))),
                             reads=[wb, hTb[t // 2]], writes=[PSb[bk]])
                for t in (t0, t0 + 1):
                    bk = vb_[t]
                    ts = slice(t * 128, (t + 1) * 128)
                    o, ob = ev_pool.get()
                    P.op("dve", (lambda e: e.tensor_copy(out=o[:, 0:ncols], in_=PS[bk][:, 0:ncols])),
                         reads=[PSb[bk]], writes=[ob])
                    dma(vS[hh0:hh0 + nh, :, ts].rearrange("h p d -> p h d"),
                        o[:, 0:ncols].rearrange("p (h d) -> p h d", h=nh), reads=[ob],
                        writes=[vSB[hh0 + i] for i in range(nh)])

        def fgate(wv, wb):
            for h in range(4):
                prev = None
                for tc in range(NQ):
                    sl = slice(tc * 512, (tc + 1) * 512)
                    bk = gbank()
                    for c in range(16):
                        P.op("pe", (lambda e: e.matmul(PS[bk][0:1, :], wv[:, c, h:h + 1], hT[:, c, sl],
                                                       start=(c == 0), stop=(c == 15))),
                             reads=[wb] + hb(tc), writes=[PSb[bk]])
                    r1, r1b = row_pool.get()
                    P.op("act", (lambda e: e.activation(out=r1, in_=PS[bk][0:1, :], func=AF.Exp,
                                                        bias=foxb[0:1, 4 + h:5 + h], scale=-1.0)),
                         reads=[PSb[bk], gB], writes=[r1b])
                    r2, r2b = row_pool.get()
                    P.op("act", (lambda e: e.activation(out=r2, in_=r1, func=AF.Ln, bias=1.0)),
                         reads=[r1b], writes=[r2b])
                    cn, cnb = cn_pool.get()
                    init = 0.0 if prev is None else prev[0][0:1, 511:512]
                    P.op("dve", (lambda e: e.tensor_tensor_scan(out=cn, data0=onef[0:1, :], data1=r2, initial=init,
                                                                op0=ALU.mult, op1=ALU.add)),
                         reads=[r2b, cB] + ([prev[1]] if prev else []), writes=[cnb])
                    prev = (cn, cnb)
                    rb, rbb = rb_pool.get()
                    P.op("dve", (lambda e: e.tensor_scalar(out=rb, in0=cn, scalar1=-1.0, scalar2=None,
                                                           op0=ALU.mult)), reads=[cnb], writes=[rbb])
                    dma(cqS[h:h + 1, sl], rb, reads=[rbb], writes=[cqSB])
                    bk2 = gbank()
                    for i in range(4):
                        P.op("pe", (lambda e: e.transpose(PS[bk2][:, i:i + 1], cn[0:1, i * 128:(i + 1) * 128],
                                                          onef[0:1, 0:1])), reads=[cnb, cB], writes=[PSb[bk2]])
                    o0 = h * NT + tc * 4
                    P.op("dve", (lambda e: e.tensor_copy(out=cnegcol[:, o0:o0 + 4], in_=PS[bk2][:, 0:4])),
                         reads=[PSb[bk2]], writes=[cnB])

        mix_c0 = {1: 1088, 2: 2624, 3: 4164}
        for m_ in (1, 2, 3):
            c0 = mix_c0[m_]
            for part in range(2):
                sc = SC128 if part == 0 else 1.0
                if m_ == 1:
                    for h in range(4):
                        hh = 4 * m_ + h
                        cs = c0 + part * 512 + h * 128

                        def ld(cs=cs):
                            return wtile_rot(win[:, :, cs:cs + 128], 16, 128, 64)

                        def cp(hd, hh=hh, part=part, sc=sc):
                            wv, wb = hd
                            dst, dstB = (qT[hh], qTB[hh]) if part == 0 else (kT[hh], kTB[hh])
                            rope_fm(wv[:, :, 0:128], wv[:, :, 128:256], (wb, wb), 16, 0, 128, hT, hb, c_cosB, c_sinB,
                                    dst, dstB, sc)
                        steps.append((ld, cp))
                else:
                    for t in range(2):
                        cs = c0 + part * 512 + t * 256

                        def ld(cs=cs):
                            wv, wb, _ = wtile(win[:, :, cs:cs + 256], 16, 256)
                            return wv, wb

                        def cp(hd, t=t, part=part, sc=sc, m_=m_):
                            wv, wb = hd
                            for b in range(2):
                                hh = 4 * m_ + t * 2 + b
                                dst, dstB = (qT[hh], qTB[hh]) if part == 0 else (kT[hh], kTB[hh])
                                gemm_fm(wv, wb, 16, b * 128, 128, hT, hb, store_bf(dst, dstB, None if part else sc))
                        steps.append((ld, cp))
            for t in range(2):
                cs = c0 + 1024 + t * 256

                def ld(cs=cs):
                    wv, wb, _ = wtile(win[:, :, cs:cs + 256], 16, 256)
                    return wv, wb

                def cp(hd, t=t, m_=m_):
                    v_gemm(hd[0], hd[1], 256, 4 * m_ + 2 * t, 2)
                steps.append((ld, cp))
            if m_ == 2:
                def ld():
                    wv, wb, _ = wtile(win[:, :, 4160:4164], 16, 4)
                    return wv, wb

                def cp(hd):
                    fgate(hd[0], hd[1])
                steps.append((ld, cp))
        pipelined(steps)


        wuq = a0(0, 3072).rearrange("p (c n) -> p c n", c=4)
        wuqr = a0(3072, 1024).rearrange("p (c n) -> p c n", c=4)
        wukv = a0(4096, 4096).rearrange("p (c n) -> p c n", c=4)
        wukvv = a0(8192, 2048).rearrange("p (c n) -> p c n", c=4)
        mwB = Buf("mlaw")
        latc = [f32v(a0(10240 + i * 8192, 8192)).rearrange("p (c s) -> p c s", c=8) for i in range(2)]
        latB = [Buf("lat0"), Buf("lat1")]
        sq4 = a0(26624, 2048).rearrange("p (c s) -> p c s", c=4)
        sq4B = Buf("sq4")
        xn = [a0(28672 + i * 2048, 2048).rearrange("p (c s) -> p c s", c=4) for i in range(2)]
        xnB = [Buf("xnq"), Buf("xnkv")]
        a0_switch([mwB, sq4B] + latB + xnB)
        st = load_stage(w_uq[l].rearrange("(c p) n -> p c n", p=128), 4, 768)
        cast(wuq, st, [stageB], [mwB])
        st4 = st.rearrange("p c (h j) -> p c h j", h=4)
        wuqr4 = wuqr.rearrange("p c (h j) -> p c h j", h=4)
        for h in range(4):
            cast(wuqr4[:, :, h, 0:32], st4[:, :, h, 160:192], [stageB], [mwB], -1.0)
            cast(wuqr4[:, :, h, 32:64], st4[:, :, h, 128:160], [stageB], [mwB])
        st = load_stage(w_ukv[l].rearrange("(c p) n -> p c n", p=128), 4, 1024)
        cast(wukv, st, [stageB], [mwB])
        st4 = st.rearrange("p c (h j) -> p c h j", h=4)
        wukvv4 = wukvv.rearrange("p c (h j) -> p c h j", h=4)
        for h in range(4):
            cast(wukvv4[:, :, h, :], st4[:, :, h, 128:256], [stageB], [mwB])
        latv = latT.rearrange("f p s -> p f s")
        xn2 = [[a0(28672 + (pp * 2 + i) * 2048, 2048).rearrange("p (c s) -> p c s", c=4) for i in range(2)] for pp in range(2)]
        xnB2 = [[Buf("xn%d%d" % (pp, i)) for i in range(2)] for pp in range(2)]
        alias([b for r_ in xnB2 for b in r_], xnB)
        a0_cur[0] = a0_cur[0] + [b for r_ in xnB2 for b in r_]

        def mla_prep(tc):
            sl = slice(tc * 512, (tc + 1) * 512)
            xn, xnB = xn2[tc % 2], xnB2[tc % 2]
            lc, lb = latc[tc % 2], latB[tc % 2]
            dma(lc, latv[:, :, sl], reads=[latTB], writes=[lb])
            for g in range(2):
                P.op("act", (lambda e: e.activation(out=sq4, in_=lc[:, g * 4:(g + 1) * 4, :], func=AF.Square)),
                     reads=[lb], writes=[sq4B])
                bk = 7
                for c in range(4):
                    P.op("pe", (lambda e: e.matmul(PS[bk][:], ones[:], sq4[:, c, :], start=(c == 0), stop=(c == 3))),
                         reads=[sq4B, cB], writes=[PSb[bk]])
                rs, rsb = rstd_from(bk, 1.0 / 512, 512)
                for c in range(4):
                    gi = 48 + g * 4 + c
                    P.op("dve", (lambda e: e.scalar_tensor_tensor(
                        out=xn[g][:, c, :], in0=lc[:, g * 4 + c, :], scalar=gains[:, gi:gi + 1],
                        in1=rs, op0=ALU.mult, op1=ALU.mult)), reads=[lb, rsb, gB], writes=[xnB[g]])

        def mla_proj(tc):
            sl = slice(tc * 512, (tc + 1) * 512)
            xn, xnB = xn2[tc % 2], xnB2[tc % 2]
            for h in range(4):
                bk = gbank()
                for c in range(4):
                    P.op("pe", (lambda e: e.matmul(PS[bk][:], wuq[:, c, h * 192:h * 192 + 128], xn[0][:, c, :],
                                                   start=(c == 0), stop=(c == 3))),
                         reads=[mwB, xnB[0]], writes=[PSb[bk]])
                store_bf(qT[h], qTB[h], SC192)(tc, sl, bk)
                b1, b2 = gbank(), gbank()
                for (bk_, w_, c0_) in ((b1, wuq, h * 192 + 128), (b2, wuqr, h * 64)):
                    for c in range(4):
                        P.op("pe", (lambda e: e.matmul(PS[bk_][0:64, :], w_[:, c, c0_:c0_ + 64], xn[0][:, c, :],
                                                       start=(c == 0), stop=(c == 3))),
                             reads=[mwB, xnB[0]], writes=[PSb[bk_]])
                rope_evac(b1, b2, 64, sl, c_cosM, c_sinM, qrT[h], qrTB, SC192)
            for h in range(4):
                bk = gbank()
                for c in range(4):
                    P.op("pe", (lambda e: e.matmul(PS[bk][:], wukv[:, c, h * 256:h * 256 + 128], xn[1][:, c, :],
                                                   start=(c == 0), stop=(c == 3))),
                         reads=[mwB, xnB[1]], writes=[PSb[bk]])
                store_bf(kT[h], kTB[h])(tc, sl, bk)
            for t in range(4):
                tt = tc * 4 + t
                bk = gbank()
                for c in range(4):
                    P.op("pe", (lambda e: e.matmul(PS[bk][:], xn[1][:, c, t * 128:(t + 1) * 128], wukvv[:, c, :],
                                                   start=(c == 0), stop=(c == 3))),
                         reads=[mwB, xnB[1]], writes=[PSb[bk]])
                o, ob = ev_pool.get()
                P.op("dve", (lambda e: e.tensor_copy(out=o, in_=PS[bk][:])), reads=[PSb[bk]], writes=[ob])
                dma(vS[0:4, :, tt * 128:(tt + 1) * 128].rearrange("h p d -> p h d"),
                    o.rearrange("p (h d) -> p h d", h=4), reads=[ob], writes=vSB[0:4])


        mla_prep(0)
        for tc in range(NQ):
            if tc + 1 < NQ:
                mla_prep(tc + 1)
            mla_proj(tc)

        def slotv(s, i):
            return a0(s * 4 * S + i * S, S)
        slotB = [[Buf("sl%d_%d" % (s, i)) for i in range(4)] for s in range(2)]
        obuf = [f32v(a0(8 * S + h * 2 * S, 2 * S)) for h in range(4)]
        obufB = [Buf("ob%d" % h) for h in range(4)]
        a0_switch([b for s_ in slotB for b in s_] + obufB)
        kpe = A1[0:64, 0:S]
        kpeB = stageB
        dma(kpe, kpeT, reads=[kpeTB], writes=[kpeB])
        blk_ctr = [0]
        drot = [0]
        for m_ in range(4):
            NST = 6 if m_ == 3 else 3

            def head_views(h):
                s = (4 * m_ + h) % 2
                return tuple(slotv(s, i) for i in range(4)), slotB[s]

            def load_head(h):
                hh = 4 * m_ + h
                (QT, KTt, Vt, EX), (qb, kb, vb, eb) = head_views(h)
                dma(QT, qT[hh], reads=[qTB[hh]], writes=[qb])
                dma(KTt, kT[hh], reads=[kTB[hh]], writes=[kb])
                dma(Vt, vS[hh], reads=[vSB[hh]], writes=[vb])
                if m_ == 0:
                    dma(EX[0:64, :], qrT[h], reads=[qrTB], writes=[eb])
                if m_ == 2:
                    dma(EX[0:1, :], cqS[h:h + 1, :], reads=[cqSB], writes=[eb])

            def crange(j, kt):
                d0 = 512 * j - 128 * kt
                if m_ == 3:
                    return 0, 512
                c0 = max(0, -d0)
                c1 = 512
                if m_ == 1:
                    c1 = max(128, min(512, 128 + 2048 - d0))
                return c0, c1

            tiles = []
            head_start = {}
            for h in range(4):
                head_start[h] = len(tiles)
                for j in range(NQ):
                    if m_ == 1:
                        kts = [kt for kt in range(0, 4 * j + 4) if 512 * j - 128 * kt <= 2048]
                    else:
                        kts = list(range(0, 4 * j + 4))
                    if m_ == 3:
                        kts = kts[::-1]
                    else:
                        kts = sorted(kts, key=lambda kt: 0 if crange(j, kt) == (0, 512) else 1)
                        assert crange(j, kts[0]) == (0, 512)
                    bi = blk_ctr[0]
                    blk_ctr[0] += 1
                    for idx, kt in enumerate(kts):
                        tiles.append(dict(h=h, j=j, idx=idx, kt=kt, nk=len(kts), ob=4 + bi % 2, dn=6 + bi % 2))
            Rs = {}

            def st_scores(T):
                h, j, kt = T["h"], T["j"], T["kt"]
                (QT, KTt, Vt, EX), (qb, kb, vb, eb) = head_views(h)
                ks = slice(kt * 128, (kt + 1) * 128)
                d0 = 512 * j - 128 * kt
                c0, c1 = crange(j, kt)
                qa = slice(j * 512 + c0, j * 512 + c1)
                if m_ == 3:
                    sb_ = (0, 1, 2, 3, 6, 7)[drot[0] % 6]
                    drot[0] += 1
                else:
                    sb_ = gbank(4, 0)
                mm = [(KTt[:, ks], QT[:, qa], [kb, qb])]
                if m_ == 0:
                    mm.append((kpe[:, ks], EX[0:64, qa], [kpeB, eb]))
                if m_ == 2:
                    mm.append((ones[0:1, :], EX[0:1, qa], [cB, eb]))
                if m_ == 1:
                    mm.append((ident[:], maskB[:, d0 + 384 + c0:d0 + 384 + c1], [cB]))
                elif d0 < 128:
                    mk = maskD if m_ == 3 else maskA
                    mm.append((ident[:], mk[:, d0 + 384 + c0:d0 + 384 + c1], [cB]))
                n_ = len(mm)
                for i_, (lt, rh, rb_) in enumerate(mm):
                    P.op("pe", (lambda e: e.matmul(PS[sb_][:, c0:c1], lt, rh, start=(i_ == 0),
                                                   stop=(m_ != 3 and i_ == n_ - 1), skip_group_check=(m_ == 3))),
                         reads=rb_, writes=[PSb[sb_]])
                T["sb"] = sb_
                T["c"] = (c0, c1)

            def finalize(T):
                h, j = T["h"], T["j"]
                qs = slice(j * 512, (j + 1) * 512)
                ob_, dn_ = T["ob"], T["dn"]
                if m_ != 3:
                    rd, rdb = tm_pool.get()
                    P.op("dve", (lambda e: e.reciprocal(out=rd, in_=PS[dn_][:])), reads=[PSb[dn_]], writes=[rdb])
                    P.op("dve", (lambda e: e.tensor_tensor(out=obuf[h][:, qs], in0=PS[ob_][:], in1=rd, op=ALU.mult)),
                         reads=[PSb[ob_], rdb], writes=[obufB[h]])
                else:
                    P.op("dve", (lambda e: e.tensor_copy(out=obuf[h][:, qs], in_=PS[ob_][:])),
                         reads=[PSb[ob_]], writes=[obufB[h]])

            def st_exp(T):
                sb_ = T["sb"]
                c0, c1 = T["c"]
                pt, ptb = pt_pool.get()
                if m_ == 2:
                    ci = T["h"] * NT + T["kt"]
                    P.op("act", (lambda e: e.activation(out=pt[:, c0:c1], in_=PS[sb_][:, c0:c1], func=AF.Exp,
                                                        bias=cnegcol[:, ci:ci + 1])),
                         reads=[PSb[sb_], cnB], writes=[ptb])
                else:
                    P.op("act", (lambda e: e.activation(out=pt[:, c0:c1], in_=PS[sb_][:, c0:c1], func=AF.Exp)),
                         reads=[PSb[sb_]], writes=[ptb])
                T["pt"] = (pt, ptb)

            def st_pv(T):
                (QT, KTt, Vt, EX), (qb, kb, vb, eb) = head_views(T["h"])
                V3 = Vt.rearrange("p (t d) -> p t d", d=128)
                pt, ptb = T["pt"]
                c0, c1 = T["c"]
                kt = T["kt"]
                first, last = (T["idx"] == 0), (T["idx"] == T["nk"] - 1)
                ob_, dn_ = T["ob"], T["dn"]
                P.op("pe", (lambda e: e.matmul(PS[ob_][:, c0:c1], V3[:, kt, :], pt[:, c0:c1], start=first, stop=last,
                                               skip_group_check=True)), reads=[vb, ptb], writes=[PSb[ob_]])
                if m_ != 3:
                    acc = tb_pool.v[T["dn"] % 2].bitcast(BF16)[:, 0:512]
                    accB = tb_pool.b[T["dn"] % 2]
                    if first:
                        P.op("dve", (lambda e: e.tensor_copy(out=acc, in_=pt)), reads=[ptb], writes=[accB])
                    else:
                        P.op("dve", (lambda e: e.tensor_tensor(out=acc[:, c0:c1], in0=acc[:, c0:c1], in1=pt[:, c0:c1],
                                                               op=ALU.add)), reads=[ptb, accB], writes=[accB])
                    if last:
                        P.op("pe", (lambda e: e.matmul(PS[dn_][:], ones[:], acc, start=True, stop=True)),
                             reads=[cB, accB], writes=[PSb[dn_]])
                if last:
                    finalize(T)

            def st_E(T):
                sb_ = T["sb"]
                ee, eeb = xf_pool.get()
                P.op("act", (lambda e: e.activation(out=ee, in_=PS[sb_][:], func=AF.Exp)),
                     reads=[PSb[sb_]], writes=[eeb])
                T["ee"] = (ee, eeb)

            def st_L(T):
                ee, eeb = T["ee"]
                ln_, lnb = ln_pool.get()
                P.op("act", (lambda e: e.activation(out=ln_, in_=ee, func=AF.Ln, bias=1.0)), reads=[eeb], writes=[lnb])
                T["ln"] = (ln_, lnb)
                key = (T["h"], T["j"], T["idx"])
                if T["idx"] < T["nk"] - 1:
                    Rn, Rnb = rr_pool.get()
                    if T["idx"] == 0:
                        P.op("dve", (lambda e: e.tensor_copy(out=Rn, in_=ln_)), reads=[lnb], writes=[Rnb])
                    else:
                        Rp, Rpb = Rs[(T["h"], T["j"], T["idx"] - 1)]
                        P.op("dve", (lambda e: e.tensor_tensor(out=Rn, in0=Rp, in1=ln_, op=ALU.add)),
                             reads=[lnb, Rpb], writes=[Rnb])
                    Rs[key] = (Rn, Rnb)

            def st_tri(T):
                sb_ = T["sb"]
                ln_, lnb = T["ln"]
                first = T["idx"] == 0
                P.op("pe", (lambda e: e.matmul(PS[sb_][:], negtri[:], ln_, start=False, stop=first,
                                               skip_group_check=True)), reads=[cB, lnb], writes=[PSb[sb_]])
                if not first:
                    Rp, Rpb = Rs[(T["h"], T["j"], T["idx"] - 1)]
                    P.op("pe", (lambda e: e.matmul(PS[sb_][:], negones[:], Rp, start=False, stop=True,
                                                   skip_group_check=True)), reads=[cB, Rpb], writes=[PSb[sb_]])

            def st_A(T):
                sb_ = T["sb"]
                pt, ptb = pt_pool.get()
                P.op("act", (lambda e: e.activation(out=pt, in_=PS[sb_][:], func=AF.Exp)), reads=[PSb[sb_]], writes=[ptb])
                T["pt"] = (pt, ptb)

            stages = [st_scores, st_E, st_L, st_tri, st_A, st_pv] if m_ == 3 else [st_scores, st_exp, st_pv]
            load_head(0)
            load_head(1)
            load_at = {head_start[h - 1] + NST - 1: h for h in (2, 3)}
            for t in range(len(tiles) + NST - 1):
                if t in load_at:
                    load_head(load_at[t])
                for st in reversed(range(NST)):
                    ti = t - st
                    if 0 <= ti < len(tiles):
                        stages[st](tiles[ti])
            for tc in range(NQ):
                sl = slice(tc * 512, (tc + 1) * 512)
                bk = gbank(4, 0)
                for h in range(4):
                    s4, s4b = ln_pool.get()
                    P.op("act", (lambda e: e.activation(out=s4, in_=obuf[h][:, sl], func=AF.Square)),
                         reads=[obufB[h]], writes=[s4b])
                    P.op("pe", (lambda e: e.matmul(PS[bk][:], ones[:], s4, start=(h == 0), stop=(h == 3))),
                         reads=[s4b, cB], writes=[PSb[bk]])
                rs, rsb = rstd_from(bk, 1.0 / 512, 512)
                for h in range(4):
                    o, ob = ev_pool.get()
                    gi = 32 + (4 * m_ + h)
                    P.op("dve", (lambda e: e.scalar_tensor_tensor(out=o, in0=obuf[h][:, sl], scalar=gains[:, gi:gi + 1],
                                                                  in1=rs, op0=ALU.mult, op1=ALU.mult)),
                         reads=[obufB[h], rsb, gB], writes=[ob])
                    dma(attnT[4 * m_ + h][:, sl], o, reads=[ob], writes=[attnTB[4 * m_ + h]])

        a0_switch(hTb)
        attv = attnT.rearrange("c p s -> p c s")
        for tc in range(NQ):
            sl = slice(tc * 512, (tc + 1) * 512)
            dma(hT[:, :, sl], attv[:, :, sl], reads=attnTB, writes=hb(tc))
        wo = w_out[l].rearrange("(c p) n -> p c n", p=128)
        steps = []
        for t in range(8):
            def ld(t=t):
                wv, wb, _ = wtile(wo[:, :, t * 256:(t + 1) * 256], 16, 256)
                return wv, wb

            def cp(hd, t=t):
                for b in range(2):
                    gemm_fm(hd[0], hd[1], 16, b * 128, 128, hT, hb, resid_evac(t * 2 + b, xsrc))
            steps.append((ld, cp))
        pipelined(steps)

        norm_to_hT(yT, 16)

        wg = w_gate[l].rearrange("(c p) n -> p c n", p=128)
        wu = w_up[l].rearrange("(c p) n -> p c n", p=128)
        steps = []
        for fb in range(44):
            def ld(fb=fb):
                return wtile2(wg[:, :, fb * 128:(fb + 1) * 128], wu[:, :, fb * 128:(fb + 1) * 128], 16, 128)

            def cp(hd, fb=fb):
                wv, wb = hd
                for tc0 in range(0, NQ, 2):
                  tcs = [t_ for t_ in (tc0, tc0 + 1) if t_ < NQ]
                  ch_ = [(tc, gbank(8, 0), gbank(8, 0)) for tc in tcs]
                  for c in range(16):
                      for (tc, bg, bu) in ch_:
                          sl = slice(tc * 512, (tc + 1) * 512)
                          for (bk, c0_) in ((bg, 0), (bu, 128)):
                              P.op("pe", (lambda e: e.matmul(PS[bk][:], wv[:, c, c0_:c0_ + 128], hT[:, c, sl],
                                                             start=(c == 0), stop=(c == 15))),
                                   reads=[wb] + hb(tc), writes=[PSb[bk]])
                  for (tc, bg, bu) in ch_:
                    sl = slice(tc * 512, (tc + 1) * 512)
                    sg, sgb = xf_pool.get()
                    P.op("act", (lambda e: e.activation(out=sg, in_=PS[bg][:], func=AF.Silu)),
                         reads=[PSb[bg]], writes=[sgb])
                    o, ob = ev_pool.get()
                    P.op("dve", (lambda e: e.tensor_tensor(out=o, in0=PS[bu][:], in1=sg, op=ALU.mult)),
                         reads=[PSb[bu], sgb], writes=[ob])
                    dma(actT[fb][:, sl], o, reads=[ob], writes=[actTB[fb]])
            steps.append((ld, cp))
        pipelined(steps)

        PAN = min(1024, S)
        npan = S // PAN
        nch = PAN // 512
        aP = a0(0, 44 * PAN).rearrange("p (c s) -> p c s", c=44)
        aPB = [Buf("aP%d" % i) for i in range(4)]
        stage2B = Buf("stage2")
        a0_switch(aPB + [stage2B])
        stage_list.append((f32v(a0(44 * PAN, 11264)), stage2B))
        xres = [f32v(a0(44 * PAN + 11264 + i * 1024, 1024)) for i in range(8)]
        xresB = [Buf("xres%d" % i) for i in range(8)]
        alias(xresB, [stage2B])
        a0_cur[0] = a0_cur[0] + xresB
        xpre = {}
        actv = actT.rearrange("c p s -> p c s")
        wd = w_down[l].rearrange("(c p) n -> p c n", p=128)
        steps = []
        for pn in range(npan):
            ps_ = slice(pn * PAN, (pn + 1) * PAN)
            for cg in range(4):
                for kg in range(4):
                    def ld(pn=pn, cg=cg, kg=kg, ps_=ps_):
                        if cg == 0 and pn == 0:
                            dma(aP[:, kg * 11:(kg + 1) * 11, :], actv[:, kg * 11:(kg + 1) * 11, ps_],
                                reads=actTB[kg * 11:(kg + 1) * 11], writes=[aPB[kg]])
                        wv, wb, _ = wtile(wd[:, kg * 11:(kg + 1) * 11, cg * 512:(cg + 1) * 512], 11, 512)
                        return wv, wb

                    def cp(hd, pn=pn, cg=cg, kg=kg):
                        wv, wb = hd
                        if kg == 2:
                            for fbi in range(4):
                                for ch in range(nch):
                                    bk = fbi * nch + ch
                                    sl = slice(pn * PAN + ch * 512, pn * PAN + (ch + 1) * 512)
                                    xpre[bk] = resid_load(cg * 4 + fbi, yT, sl, (xres[bk], xresB[bk]))
                        for k in range(11):
                            kk_ = kg * 11 + k
                            for fbi in range(4):
                                for ch in range(nch):
                                    bk = fbi * nch + ch
                                    P.op("pe", (lambda e: e.matmul(PS[bk][:], wv[:, k, fbi * 128:(fbi + 1) * 128],
                                                                   aP[:, kk_, ch * 512:(ch + 1) * 512],
                                                                   start=(kk_ == 0), stop=(kk_ == 43))),
                                         reads=[wb, aPB[kg]], writes=[PSb[bk]])
                        if cg == 3 and pn + 1 < npan:
                            pn1 = slice((pn + 1) * PAN, (pn + 2) * PAN)
                            dma(aP[:, kg * 11:(kg + 1) * 11, :], actv[:, kg * 11:(kg + 1) * 11, pn1],
                                reads=actTB[kg * 11:(kg + 1) * 11], writes=[aPB[kg]])
                        if kg == 3:
                            for fbi in range(4):
                                for ch in range(nch):
                                    bk = fbi * nch + ch
                                    fb = cg * 4 + fbi
                                    sl = slice(pn * PAN + ch * 512, pn * PAN + (ch + 1) * 512)
                                    xo, xob = xpre.pop(bk)
                                    resid_add_store(fb, sl, bk, xo, xob)
                    steps.append((ld, cp))
        pipelined(steps)
        stage_list.pop()

    def fin_dst(i, sl, rs, rsb, xc, xcB):
        for c in range(16):
            o, ob = xf_pool.get()
            P.op("dve", (lambda e: e.scalar_tensor_tensor(
                out=o[:, 0:256], in0=xc[:, c, :], scalar=gains[:, 56 + c:57 + c], in1=rs[:, 0:256],
                op0=ALU.mult, op1=ALU.mult)), reads=[xcB, rsb, gB], writes=[ob])
            dma(yT[c * 128:(c + 1) * 128, sl], o[:, 0:256], reads=[ob], writes=[yTB[c]])
    norm_phase(yT, fin_dst)

    P.finish()
    with nc.allow_non_contiguous_dma(reason="small strided loads"):
        P.emit()
    es.close()
    return nc


WNAMES = ["attn_norm", "w_in", "mla_q_norm", "w_uq", "mla_kv_norm", "w_ukv", "fox_forget_bias", "group_norm",
          "w_out", "ffn_norm", "w_gate", "w_up", "w_down", "final_norm"]


def run(inputs, S, DEPTH, ncores):
    nc = build(S, DEPTH)
    consts = host_consts(S)
    x = np.asarray(inputs["x"], dtype=np.float32)
    shared = {k: np.ascontiguousarray(np.asarray(inputs[k], dtype=np.float32)) for k in WNAMES}
    shared.update(consts)
    in_maps = []
    for b in range(ncores):
        m = dict(shared)
        m["xT"] = np.ascontiguousarray(x[b].T)
        in_maps.append(m)
    res = run_bass_kernel_spmd(nc, in_maps, core_ids=list(range(ncores)))
    out = np.stack([np.ascontiguousarray(r["yT"].T) for r in res.results], axis=0)
    return out.astype(np.float32)


def kernel(**inputs):
    x = inputs["x"]
    return run(inputs, x.shape[1], inputs["w_in"].shape[0], x.shape[0])
```

```python
import math
from contextlib import ExitStack
import numpy as np
import ml_dtypes
import concourse.bass as bass
import concourse.mybir as mybir
from concourse.bass_utils import run_bass_kernel_spmd

F32 = mybir.dt.float32
BF16 = mybir.dt.bfloat16
AF = mybir.ActivationFunctionType
ALU = mybir.AluOpType

D = 2048
KC = 16
FF = 5632
INW = 5700
NEG = -30000.0
EPS = 1e-6
NDS = 24


class Buf:
    __slots__ = ("name", "w", "r")

    def __init__(self, name):
        self.name = name
        self.w = None
        self.r = {}


class _Rec:
    def __getattr__(self, name):
        return lambda *a, **k: (name, a, k)


_REC = _Rec()


def alias(new_bufs, old_bufs):
    toks = {}
    for b in old_bufs:
        for t in ([b.w] if b.w is not None else []) + list(b.r.values()):
            if toks.get(t[0], 0) < t[1]:
                toks[t[0]] = t[1]
    for nb in new_bufs:
        nb.w = None
        nb.r = {("al", i): (k, v) for i, (k, v) in enumerate(toks.items())}


class Prog:
    def __init__(self, nc, es):
        self.nc = nc
        self.streams = {k: [] for k in ("pe", "act", "dve", "pool", "sp")}
        self.count = {k: 0 for k in self.streams}
        self.sem = {k: es.enter_context(nc.semaphore("s_" + k)) for k in ("pe", "act", "dve", "pool")}
        self.dsem = [es.enter_context(nc.semaphore("d%d" % i)) for i in range(NDS)]
        self.dcount = 0
        self.waited = {k: {} for k in self.streams}

    def op(self, eng, fn, reads=(), writes=()):
        toks = []
        for b in reads:
            if b.w is not None:
                toks.append(b.w)
        for b in writes:
            if b.w is not None:
                toks.append(b.w)
            toks.extend(b.r.values())
        if eng == "sp":
            j = self.dcount % NDS
            n = self.dcount // NDS + 1
            self.dcount += 1
            key = ("d", j)
            tok = (key, 16 * n)
            if n > 1:
                toks.append((key, 16 * (n - 1)))
            rkey = key
        else:
            self.count[eng] += 1
            tok = (eng, self.count[eng])
            rkey = eng
        need = {}
        wd = self.waited[eng]
        for (k, v) in toks:
            if k == "pe" and eng == "pe":
                continue
            if wd.get(k, 0) >= v:
                continue
            if need.get(k, 0) < v:
                need[k] = v
        for k, v in need.items():
            wd[k] = v
        self.streams[eng].append((list(need.items()), fn(_REC), tok))
        for b in reads:
            b.r[rkey] = tok
        for b in writes:
            b.w = tok
            b.r = {}
        return tok

    def _sem(self, key):
        return self.dsem[key[1]] if isinstance(key, tuple) else self.sem[key]

    def finish(self):
        waits = []
        for j in range(NDS):
            n = (self.dcount - j + NDS - 1) // NDS if self.dcount > j else 0
            if n > 0:
                waits.append((("d", j), 16 * n))
        self.streams["sp"].append((waits, None, None))

    def emit(self):
        nc = self.nc
        with nc.Block() as block:
            def run(e, key):
                for waits, fn, tok in self.streams[key]:
                    for (k, v) in waits:
                        e.wait_ge(self._sem(k), v)
                    if fn is None:
                        continue
                    inst = getattr(e, fn[0])(*fn[1], **fn[2])
                    inst.then_inc(self._sem(tok[0]), 16 if isinstance(tok[0], tuple) else 1)

            @block.tensor
            def _(e):
                run(e, "pe")

            @block.scalar
            def _(e):
                run(e, "act")

            @block.vector
            def _(e):
                run(e, "dve")

            @block.sync
            def _(e):
                run(e, "sp")


class Pool:
    def __init__(self, views, name):
        self.v = views
        self.b = [Buf("%s%d" % (name, i)) for i in range(len(views))]
        self.i = 0

    def get(self):
        i = self.i
        self.i = (i + 1) % len(self.v)
        return self.v[i], self.b[i]


def host_consts(S):
    bf = ml_dtypes.bfloat16
    kk = np.arange(128)[:, None]
    c = {}
    c["ident"] = np.eye(128, dtype=np.float32).astype(bf)
    c["ones"] = np.ones((128, 128), np.float32).astype(bf)
    c["negones"] = (-np.ones((128, 128), np.float32)).astype(bf)
    jj = np.arange(128)[:, None]
    k2 = np.arange(128)[None, :]
    c["negtri"] = np.where(jj >= k2, -1.0, 0.0).astype(np.float32).astype(bf)
    cc = np.arange(896)[None, :]
    dl = cc - kk - 384
    c["maskA"] = np.where(dl >= 0, 0.0, NEG).astype(np.float32).astype(bf)
    c["maskD"] = np.where(dl >= 1, 0.0, NEG).astype(np.float32).astype(bf)
    cc = np.arange(3072)[None, :]
    dl = cc - kk - 384
    mult = ((dl >= 0) & (dl <= 128)).astype(np.int32) + ((dl >= 0) & (dl % 4 == 0) & (dl <= 512)) \
        + ((dl >= 0) & (dl % 16 == 0) & (dl <= 2048))
    c["maskB"] = np.where(mult > 0, np.log(np.maximum(mult, 1).astype(np.float32)), NEG).astype(np.float32).astype(bf)
    pos = np.arange(S, dtype=np.float32)

    def tab(dim):
        inv = np.float32(10000.0) ** (-(np.arange(0, dim, 2, dtype=np.float32) / np.float32(dim)))
        ang = (pos[:, None] * inv[None, :]).astype(np.float32)
        co = np.cos(ang).astype(np.float32).T
        si = np.sin(ang).astype(np.float32).T
        return (np.ascontiguousarray(np.concatenate([co, co], 0)),
                np.ascontiguousarray(np.concatenate([si, si], 0)))

    c["cosB"], c["sinB"] = tab(128)
    c["cosM"], c["sinM"] = tab(64)
    c["onef"] = np.ones((1, 512), np.float32)
    return c


def build(S, DEPTH):
    NT = S // 128
    NQ = S // 512
    NC2 = S // 256
    nc = bass.Bass("TRN2", target_bir_lowering=False, dynamic_dma_scratch_size=256)
    es = ExitStack()

    def din(name, shape, dt=F32):
        return nc.dram_tensor(name, list(shape), dt, kind="ExternalInput").ap()

    def dscr(name, shape, dt):
        return nc.dram_tensor(name, list(shape), dt, kind="Internal").ap()

    xT = din("xT", [D, S])
    attn_norm = din("attn_norm", [DEPTH, D])
    w_in = din("w_in", [DEPTH, D, INW])
    mla_q_norm = din("mla_q_norm", [DEPTH, 512])
    w_uq = din("w_uq", [DEPTH, 512, 768])
    mla_kv_norm = din("mla_kv_norm", [DEPTH, 512])
    w_ukv = din("w_ukv", [DEPTH, 512, 1024])
    fox_b = din("fox_forget_bias", [DEPTH, 4])
    group_norm = din("group_norm", [DEPTH, D])
    w_out = din("w_out", [DEPTH, D, D])
    ffn_norm = din("ffn_norm", [DEPTH, D])
    w_gate = din("w_gate", [DEPTH, D, FF])
    w_up = din("w_up", [DEPTH, D, FF])
    w_down = din("w_down", [DEPTH, FF, D])
    final_norm = din("final_norm", [D])
    c_ident = din("ident", [128, 128], BF16)
    c_ones = din("ones", [128, 128], BF16)
    c_negones = din("negones", [128, 128], BF16)
    c_negtri = din("negtri", [128, 128], BF16)
    c_maskA = din("maskA", [128, 896], BF16)
    c_maskD = din("maskD", [128, 896], BF16)
    c_maskB = din("maskB", [128, 3072], BF16)
    c_cosB = din("cosB", [128, S])
    c_sinB = din("sinB", [128, S])
    c_cosM = din("cosM", [64, S])
    c_sinM = din("sinM", [64, S])
    c_onef = din("onef", [1, 512])
    yT = nc.dram_tensor("yT", [D, S], F32, kind="ExternalOutput").ap()

    latT = dscr("latT", [8, 128, S], F32)
    qT = dscr("qT", [16, 128, S], BF16)
    kT = dscr("kT", [16, 128, S], BF16)
    qrT = dscr("qrT", [4, 64, S], BF16)
    kpeT = dscr("kpeT", [64, S], BF16)
    vS = dscr("vS", [16, 128, NT * 128], BF16)
    cqS = dscr("cqS", [4, S], BF16)
    attnT = dscr("attnT", [16, 128, S], BF16)
    actT = dscr("actT", [44, 128, S], BF16)
    yTB = [Buf("yT%d" % i) for i in range(16)]
    latTB = Buf("latT")
    qTB = [Buf("qT%d" % i) for i in range(16)]
    kTB = [Buf("kT%d" % i) for i in range(16)]
    qrTB, kpeTB, cqSB = Buf("qrT"), Buf("kpeT"), Buf("cqS")
    vSB = [Buf("vS%d" % i) for i in range(16)]
    attnTB = [Buf("at%d" % i) for i in range(16)]
    actTB = [Buf("ac%d" % i) for i in range(44)]

    def sb(name, shape, dt):
        return es.enter_context(nc.sbuf_tensor("s_" + name, list(shape), dt))

    P = Prog(nc, es)
    A0 = sb("A0", [128, 65536], BF16)
    A1 = sb("A1", [128, 22528], BF16)
    PS = [es.enter_context(nc.psum_tensor("ps%d" % i, [128, 512], F32)) for i in range(8)]
    PSb = [Buf("ps%d" % i) for i in range(8)]

    def a0(off, n):
        return A0[:, off:off + n]

    def f32v(ap):
        return ap.bitcast(F32)

    ident = sb("ident", [128, 128], BF16)
    ones = sb("ones", [128, 128], BF16)
    negones = sb("negones", [128, 128], BF16)
    negtri = sb("negtri", [128, 128], BF16)
    maskA = sb("maskA", [128, 896], BF16)
    maskD = sb("maskD", [128, 896], BF16)
    maskB = sb("maskB", [128, 3072], BF16)
    onef = sb("onef", [1, 512], F32)
    gains = sb("gains", [128, 80], F32)
    foxb = sb("foxb", [1, 8], F32)
    cnegcol = sb("cnegcol", [128, 4 * NT], F32)
    cB, gB, cnB = Buf("consts"), Buf("gains"), Buf("cnegcol")

    def mkpool(name, n, shape, dt):
        return Pool([sb("%s%d" % (name, i), shape, dt)[:] for i in range(n)], name)

    pt_pool = mkpool("pt", 3, [128, 512], BF16)
    ev_pool = mkpool("ev", 3, [128, 512], BF16)
    xf_pool = mkpool("xf", 3, [128, 512], F32)
    tb_pool = mkpool("tb", 2, [128, 512], F32)
    tm_pool = mkpool("tm", 3, [128, 512], F32)
    ln_pool = mkpool("ln", 3, [128, 512], BF16)
    rr_pool = mkpool("rr", 3, [128, 512], BF16)
    row_pool = mkpool("row", 2, [1, 512], F32)
    cn_pool = mkpool("cn", 2, [1, 512], F32)
    rb_pool = mkpool("rb", 2, [1, 512], BF16)

    def dma(out, in_, reads=(), writes=()):
        P.op("sp", (lambda e: e.dma_start(out=out, in_=in_)), reads=reads, writes=writes)

    for (t, src) in ((ident, c_ident), (ones, c_ones), (negones, c_negones), (negtri, c_negtri),
                     (maskA, c_maskA), (maskD, c_maskD), (maskB, c_maskB), (onef, c_onef)):
        dma(t[:], src, writes=[cB])

    gemm_rot = [0]

    def gbank(n=4, base=0):
        i = base + gemm_rot[0] % n
        gemm_rot[0] += 1
        return i

    stage = f32v(A1[:, 0:11264])
    stageB = Buf("stage")
    wslots = [A1[:, 11264:16896], A1[:, 16896:22528]]
    wslotB = [Buf("w0"), Buf("w1")]
    wrot = [0]

    def load_stage(src3, kc, w):
        st = stage[:, 0:kc * w].rearrange("p (c n) -> p c n", c=kc)
        dma(st, src3, writes=[stageB])
        return st

    def wslot(kc, w):
        i = wrot[0] % 2
        wrot[0] += 1
        return wslots[i][:, 0:kc * w].rearrange("p (c n) -> p c n", c=kc), wslotB[i]

    def cast(out, in_, rb, wb, scale=None):
        if scale is None:
            P.op("act", (lambda e: e.activation(out=out, in_=in_, func=AF.Copy)), reads=rb, writes=wb)
        else:
            P.op("act", (lambda e: e.activation(out=out, in_=in_, func=AF.Copy, scale=scale)), reads=rb, writes=wb)

    stage_list = [(stage, stageB)]
    srot = [0]

    def wtile(src3, kc, w):
        sv, sB = stage_list[srot[0] % len(stage_list)]
        srot[0] += 1
        st = sv[:, 0:kc * w].rearrange("p (c n) -> p c n", c=kc)
        dma(st, src3, writes=[sB])
        wv, wb = wslot(kc, w)
        cast(wv, st, [sB], [wb])
        return wv, wb, st

    def wtile_rot(src3, kc, w, half):
        st = load_stage(src3, kc, w)
        wv, wb = wslot(kc, 2 * w)
        cast(wv[:, :, 0:w], st, [stageB], [wb])
        cast(wv[:, :, w:w + half], st[:, :, half:w], [stageB], [wb], -1.0)
        cast(wv[:, :, w + half:2 * w], st[:, :, 0:half], [stageB], [wb])
        return wv, wb

    def wtile2(srcA, srcB, kc, w):
        st = stage[:, 0:kc * 2 * w].rearrange("p (c n) -> p c n", c=kc)
        dma(st[:, :, 0:w], srcA, writes=[stageB])
        dma(st[:, :, w:2 * w], srcB, writes=[stageB])
        wv, wb = wslot(kc, 2 * w)
        cast(wv, st, [stageB], [wb])
        return wv, wb

    def pipelined(steps):
        if not steps:
            return
        nxt = steps[0][0]()
        for i, (_, comp) in enumerate(steps):
            cur = nxt
            if i + 1 < len(steps):
                nxt = steps[i + 1][0]()
            comp(cur)

    hT = a0(0, 16 * S).rearrange("p (c s) -> p c s", c=16)
    hTb = [Buf("hT%d" % i) for i in range(NC2)]
    a0_cur = [list(hTb)]

    def a0_switch(new_bufs):
        alias(new_bufs, a0_cur[0])
        a0_cur[0] = list(new_bufs)

    def hb(tc):
        return [hTb[2 * tc], hTb[2 * tc + 1]]

    xcs = [f32v(A1[:, i * 8192:(i + 1) * 8192]).rearrange("p (c s) -> p c s", c=16) for i in range(2)]
    sq = A1[:, 16384:20480].rearrange("p (c s) -> p c s", c=16)
    xcBs, sqB = [Buf("xc0"), Buf("xc1")], Buf("sqn")
    a1_perm = [stageB, wslotB[0], wslotB[1]]
    a1_norm = xcBs + [sqB]

    def rstd_from(bk, inv_n, width):
        sd, sdb = tm_pool.get()
        P.op("act", (lambda e: e.activation(out=sd[:, 0:width], in_=PS[bk][:, 0:width], func=AF.Ln,
                                            bias=EPS, scale=inv_n)), reads=[PSb[bk]], writes=[sdb])
        rs, rsb = tm_pool.get()
        P.op("act", (lambda e: e.activation(out=rs[:, 0:width], in_=sd[:, 0:width], func=AF.Exp, scale=-0.5)),
             reads=[sdb], writes=[rsb])
        return rs, rsb

    def norm_phase(src, dst_fn):
        srcv = src.rearrange("(c p) s -> p c s", p=128)
        alias(a1_norm, a1_perm)
        for i in range(NC2):
            sl = slice(i * 256, (i + 1) * 256)
            xc, xcB = xcs[i % 2], xcBs[i % 2]
            dma(xc, srcv[:, :, sl], reads=yTB, writes=[xcB])
            P.op("act", (lambda e: e.activation(out=sq, in_=xc, func=AF.Square)), reads=[xcB], writes=[sqB])
            bk = 7
            for c in range(16):
                P.op("pe", (lambda e: e.matmul(PS[bk][:, 0:256], ones[:], sq[:, c, :], start=(c == 0),
                                               stop=(c == 15))), reads=[sqB, cB], writes=[PSb[bk]])
            rs, rsb = rstd_from(bk, 1.0 / D, 256)
            dst_fn(i, sl, rs, rsb, xc, xcB)
        alias(a1_perm, a1_norm)

    def norm_to_hT(src, g0):
        def dst(i, sl, rs, rsb, xc, xcB):
            for c in range(16):
                P.op("dve", (lambda e: e.scalar_tensor_tensor(
                    out=hT[:, c, sl], in0=xc[:, c, :], scalar=gains[:, g0 + c:g0 + c + 1], in1=rs[:, 0:256],
                    op0=ALU.mult, op1=ALU.mult)), reads=[xcB, rsb, gB], writes=[hTb[i]])
        norm_phase(src, dst)

    def gemm_fm(wv, wb, kc, col0, m, actv, abf, evac, nbank=4, base=0):
        for tc0 in range(0, NQ, 2):
            tcs = [t_ for t_ in (tc0, tc0 + 1) if t_ < NQ]
            bks = [gbank(nbank, base) for _ in tcs]
            if hasattr(evac, "prefetch"):
                for tc in tcs:
                    evac.prefetch(tc, slice(tc * 512, (tc + 1) * 512))
            for c in range(kc):
                for bk, tc in zip(bks, tcs):
                    sl = slice(tc * 512, (tc + 1) * 512)
                    P.op("pe", (lambda e: e.matmul(PS[bk][0:m, :], wv[:, c, col0:col0 + m], actv[:, c, sl],
                                                   start=(c == 0), stop=(c == kc - 1))),
                         reads=[wb] + abf(tc), writes=[PSb[bk]])
            for bk, tc in zip(bks, tcs):
                evac(tc, slice(tc * 512, (tc + 1) * 512), bk)

    def store_bf(dst, dstB, scale=None, m=128):
        def evac(tc, sl, bk):
            o, ob = ev_pool.get()
            if scale is None:
                P.op("dve", (lambda e: e.tensor_copy(out=o[0:m, :], in_=PS[bk][0:m, :])), reads=[PSb[bk]], writes=[ob])
            else:
                P.op("dve", (lambda e: e.tensor_scalar(out=o[0:m, :], in0=PS[bk][0:m, :], scalar1=scale, scalar2=None,
                                                       op0=ALU.mult)), reads=[PSb[bk]], writes=[ob])
            dma(dst[0:m, sl], o[0:m, :], reads=[ob], writes=[dstB])
        return evac

    def rope_evac(b1, b2, m, sl, cosd, sind, dst, dstB, scale):
        ct, ctb = tb_pool.get()
        dma(ct[0:m, :], cosd[0:m, sl], writes=[ctb])
        st_, stb = tb_pool.get()
        dma(st_[0:m, :], sind[0:m, sl], writes=[stb])
        t1, t1b = tm_pool.get()
        P.op("dve", (lambda e: e.scalar_tensor_tensor(out=t1[0:m, :], in0=PS[b1][0:m, :], scalar=scale,
                                                      in1=ct[0:m, :], op0=ALU.mult, op1=ALU.mult)),
             reads=[PSb[b1], ctb], writes=[t1b])
        t2, t2b = tm_pool.get()
        P.op("dve", (lambda e: e.scalar_tensor_tensor(out=t2[0:m, :], in0=PS[b2][0:m, :], scalar=scale,
                                                      in1=st_[0:m, :], op0=ALU.mult, op1=ALU.mult)),
             reads=[PSb[b2], stb], writes=[t2b])
        o, ob = ev_pool.get()
        P.op("dve", (lambda e: e.tensor_tensor(out=o[0:m, :], in0=t1[0:m, :], in1=t2[0:m, :], op=ALU.add)),
             reads=[t1b, t2b], writes=[ob])
        dma(dst[0:m, sl], o[0:m, :], reads=[ob], writes=[dstB])

    def rope_fm(wv, wr, wbufs, kc, col0, m, actv, abf, cosd, sind, dst, dstB, scale):
        for tc in range(NQ):
            sl = slice(tc * 512, (tc + 1) * 512)
            b1 = gbank()
            b2 = gbank()
            for c in range(kc):
                for (bk, w_, wb_) in ((b1, wv, wbufs[0]), (b2, wr, wbufs[1])):
                    P.op("pe", (lambda e: e.matmul(PS[bk][0:m, :], w_[:, c, col0:col0 + m], actv[:, c, sl],
                                                   start=(c == 0), stop=(c == kc - 1))),
                         reads=[wb_] + abf(tc), writes=[PSb[bk]])
            rope_evac(b1, b2, m, sl, cosd, sind, dst, dstB, scale)

    SC128 = 128 ** -0.5
    SC192 = 192 ** -0.5

    def resid_load(fb, src, sl, buf=None):
        xo, xob = buf if buf is not None else xf_pool.get()
        dma(xo, src[fb * 128:(fb + 1) * 128, sl], reads=[yTB[fb]], writes=[xob])
        return xo, xob

    def resid_add_store(fb, sl, bk, xo, xob):
        P.op("dve", (lambda e: e.tensor_tensor(out=xo, in0=PS[bk][:], in1=xo, op=ALU.add)),
             reads=[PSb[bk], xob], writes=[xob])
        dma(yT[fb * 128:(fb + 1) * 128, sl], xo, reads=[xob], writes=[yTB[fb]])

    def resid_evac(fb, src):
        pre = {}

        def prefetch(tc, sl):
            pre[tc] = resid_load(fb, src, sl)

        def evac(tc, sl, bk):
            xo, xob = pre.pop(tc) if tc in pre else resid_load(fb, src, sl)
            resid_add_store(fb, sl, bk, xo, xob)
        evac.prefetch = prefetch
        return evac

    for l in range(DEPTH):
        xsrc = xT if l == 0 else yT

        def gl(dst_c0, vec, n):
            dma(gains[:, dst_c0:dst_c0 + n], vec.rearrange("(c p) -> p c", p=128), writes=[gB])
        gl(0, attn_norm[l], 16)
        gl(16, ffn_norm[l], 16)
        gl(32, group_norm[l], 16)
        gl(48, mla_q_norm[l], 4)
        gl(52, mla_kv_norm[l], 4)
        if l == 0:
            gl(56, final_norm, 16)
        dma(foxb[0:1, 0:4], fox_b[l:l + 1, :], writes=[gB])
        P.op("dve", (lambda e: e.tensor_scalar(out=foxb[0:1, 4:8], in0=foxb[0:1, 0:4], scalar1=-1.0, scalar2=None,
                                               op0=ALU.mult)), reads=[gB], writes=[gB])

        if l > 0:
            a0_switch(hTb)
        norm_to_hT(xsrc, 0)

        win = w_in[l].rearrange("(c p) n -> p c n", p=128)
        steps = []

        def lat_evac(fb):
            def evac(tc, sl, bk):
                o, ob = xf_pool.get()
                P.op("dve", (lambda e: e.tensor_copy(out=o, in_=PS[bk][:])), reads=[PSb[bk]], writes=[ob])
                dma(latT[fb][:, sl], o, reads=[ob], writes=[latTB])
            return evac

        for t in range(4):
            def ld(t=t):
                wv, wb, _ = wtile(win[:, :, t * 256:(t + 1) * 256], 16, 256)
                return wv, wb

            def cp(hd, t=t):
                wv, wb = hd
                for b in range(2):
                    gemm_fm(wv, wb, 16, b * 128, 128, hT, hb, lat_evac(t * 2 + b))
            steps.append((ld, cp))

        def ld_kr():
            return wtile_rot(win[:, :, 1024:1088], 16, 64, 32)

        def cp_kr(hd):
            wv, wb = hd
            rope_fm(wv[:, :, 0:64], wv[:, :, 64:128], (wb, wb), 16, 0, 64, hT, hb, c_cosM, c_sinM, kpeT, kpeTB, 1.0)
        steps.append((ld_kr, cp_kr))

        def v_gemm(wv, wb, ncols, hh0, nh):
            vb_ = {}
            for t0 in range(0, NT, 2):
                vb_[t0], vb_[t0 + 1] = gbank(), gbank()
                for c in range(16):
                    for t in (t0, t0 + 1):
                        bk = vb_[t]
                        ts = slice(t * 128, (t + 1) * 128)
                        P.op("pe", (lambda e: e.matmul(PS[bk][:, 0:ncols], hT[:, c, ts], wv[:, c, 0:ncols],
                                                       start=(c == 0), stop=(c == 15))),
                             reads=[wb, hTb[t // 2]], writes=[PSb[bk]])
                for t in (t0, t0 + 1):
                    bk = vb_[t]
                    ts = slice(t * 128, (t + 1) * 128)
                    o, ob = ev_pool.get()
                    P.op("dve", (lambda e: e.tensor_copy(out=o[:, 0:ncols], in_=PS[bk][:, 0:ncols])),
                         reads=[PSb[bk]], writes=[ob])
                    dma(vS[hh0:hh0 + nh, :, ts].rearrange("h p d -> p h d"),
                        o[:, 0:ncols].rearrange("p (h d) -> p h d", h=nh), reads=[ob],
                        writes=[vSB[hh0 + i] for i in range(nh)])

        def fgate(wv, wb):
            for h in range(4):
                prev = None
                for tc in range(NQ):
                    sl = slice(tc * 512, (tc + 1) * 512)
                    bk = gbank()
                    for c in range(16):
                        P.op("pe", (lambda e: e.matmul(PS[bk][0:1, :], wv[:, c, h:h + 1], hT[:, c, sl],
                                                       start=(c == 0), stop=(c == 15))),
                             reads=[wb] + hb(tc), writes=[PSb[bk]])
                    r1, r1b = row_pool.get()
                    P.op("act", (lambda e: e.activation(out=r1, in_=PS[bk][0:1, :], func=AF.Exp,
                                                        bias=foxb[0:1, 4 + h:5 + h], scale=-1.0)),
                         reads=[PSb[bk], gB], writes=[r1b])
                    r2, r2b = row_pool.get()
                    P.op("act", (lambda e: e.activation(out=r2, in_=r1, func=AF.Ln, bias=1.0)),
                         reads=[r1b], writes=[r2b])
                    cn, cnb = cn_pool.get()
                    init = 0.0 if prev is None else prev[0][0:1, 511:512]
                    P.op("dve", (lambda e: e.tensor_tensor_scan(out=cn, data0=onef[0:1, :], data1=r2, initial=init,
                                                                op0=ALU.mult, op1=ALU.add)),
                         reads=[r2b, cB] + ([prev[1]] if prev else []), writes=[cnb])
                    prev = (cn, cnb)
                    rb, rbb = rb_pool.get()
                    P.op("dve", (lambda e: e.tensor_scalar(out=rb, in0=cn, scalar1=-1.0, scalar2=None,
                                                           op0=ALU.mult)), reads=[cnb], writes=[rbb])
                    dma(cqS[h:h + 1, sl], rb, reads=[rbb], writes=[cqSB])
                    bk2 = gbank()
                    for i in range(4):
                        P.op("pe", (lambda e: e.transpose(PS[bk2][:, i:i + 1], cn[0:1, i * 128:(i + 1) * 128],
                                                          onef[0:1, 0:1])), reads=[cnb, cB], writes=[PSb[bk2]])
                    o0 = h * NT + tc * 4
                    P.op("dve", (lambda e: e.tensor_copy(out=cnegcol[:, o0:o0 + 4], in_=PS[bk2][:, 0:4])),
                         reads=[PSb[bk2]], writes=[cnB])

        mix_c0 = {1: 1088, 2: 2624, 3: 4164}
        for m_ in (1, 2, 3):
            c0 = mix_c0[m_]
            for part in range(2):
                sc = SC128 if part == 0 else 1.0
                if m_ == 1:
                    for h in range(4):
                        hh = 4 * m_ + h
                        cs = c0 + part * 512 + h * 128

                        def ld(cs=cs):
                            return wtile_rot(win[:, :, cs:cs + 128], 16, 128, 64)

                        def cp(hd, hh=hh, part=part, sc=sc):
                            wv, wb = hd
                            dst, dstB = (qT[hh], qTB[hh]) if part == 0 else (kT[hh], kTB[hh])
                            rope_fm(wv[:, :, 0:128], wv[:, :, 128:256], (wb, wb), 16, 0, 128, hT, hb, c_cosB, c_sinB,
                                    dst, dstB, sc)
                        steps.append((ld, cp))
                else:
                    for t in range(2):
                        cs = c0 + part * 512 + t * 256

                        def ld(cs=cs):
                            wv, wb, _ = wtile(win[:, :, cs:cs + 256], 16, 256)
                            return wv, wb

                        def cp(hd, t=t, part=part, sc=sc, m_=m_):
                            wv, wb = hd
                            for b in range(2):
                                hh = 4 * m_ + t * 2 + b
                                dst, dstB = (qT[hh], qTB[hh]) if part == 0 else (kT[hh], kTB[hh])
                                gemm_fm(wv, wb, 16, b * 128, 128, hT, hb, store_bf(dst, dstB, None if part else sc))
                        steps.append((ld, cp))
            for t in range(2):
                cs = c0 + 1024 + t * 256

                def ld(cs=cs):
                    wv, wb, _ = wtile(win[:, :, cs:cs + 256], 16, 256)
                    return wv, wb

                def cp(hd, t=t, m_=m_):
                    v_gemm(hd[0], hd[1], 256, 4 * m_ + 2 * t, 2)
                steps.append((ld, cp))
            if m_ == 2:
                def ld():
                    wv, wb, _ = wtile(win[:, :, 4160:4164], 16, 4)
                    return wv, wb

                def cp(hd):
                    fgate(hd[0], hd[1])
                steps.append((ld, cp))
        pipelined(steps)


        wuq = a0(0, 3072).rearrange("p (c n) -> p c n", c=4)
        wuqr = a0(3072, 1024).rearrange("p (c n) -> p c n", c=4)
        wukv = a0(4096, 4096).rearrange("p (c n) -> p c n", c=4)
        wukvv = a0(8192, 2048).rearrange("p (c n) -> p c n", c=4)
        mwB = Buf("mlaw")
        latc = [f32v(a0(10240 + i * 8192, 8192)).rearrange("p (c s) -> p c s", c=8) for i in range(2)]
        latB = [Buf("lat0"), Buf("lat1")]
        sq4 = a0(26624, 2048).rearrange("p (c s) -> p c s", c=4)
        sq4B = Buf("sq4")
        xn = [a0(28672 + i * 2048, 2048).rearrange("p (c s) -> p c s", c=4) for i in range(2)]
        xnB = [Buf("xnq"), Buf("xnkv")]
        a0_switch([mwB, sq4B] + latB + xnB)
        st = load_stage(w_uq[l].rearrange("(c p) n -> p c n", p=128), 4, 768)
        cast(wuq, st, [stageB], [mwB])
        st4 = st.rearrange("p c (h j) -> p c h j", h=4)
        wuqr4 = wuqr.rearrange("p c (h j) -> p c h j", h=4)
        for h in range(4):
            cast(wuqr4[:, :, h, 0:32], st4[:, :, h, 160:192], [stageB], [mwB], -1.0)
            cast(wuqr4[:, :, h, 32:64], st4[:, :, h, 128:160], [stageB], [mwB])
        st = load_stage(w_ukv[l].rearrange("(c p) n -> p c n", p=128), 4, 1024)
        cast(wukv, st, [stageB], [mwB])
        st4 = st.rearrange("p c (h j) -> p c h j", h=4)
        wukvv4 = wukvv.rearrange("p c (h j) -> p c h j", h=4)
        for h in range(4):
            cast(wukvv4[:, :, h, :], st4[:, :, h, 128:256], [stageB], [mwB])
        latv = latT.rearrange("f p s -> p f s")
        xn2 = [[a0(28672 + (pp * 2 + i) * 2048, 2048).rearrange("p (c s) -> p c s", c=4) for i in range(2)] for pp in range(2)]
        xnB2 = [[Buf("xn%d%d" % (pp, i)) for i in range(2)] for pp in range(2)]
        alias([b for r_ in xnB2 for b in r_], xnB)
        a0_cur[0] = a0_cur[0] + [b for r_ in xnB2 for b in r_]

        def mla_prep(tc):
            sl = slice(tc * 512, (tc + 1) * 512)
            xn, xnB = xn2[tc % 2], xnB2[tc % 2]
            lc, lb = latc[tc % 2], latB[tc % 2]
            dma(lc, latv[:, :, sl], reads=[latTB], writes=[lb])
            for g in range(2):
                P.op("act", (lambda e: e.activation(out=sq4, in_=lc[:, g * 4:(g + 1) * 4, :], func=AF.Square)),
                     reads=[lb], writes=[sq4B])
                bk = 7
                for c in range(4):
                    P.op("pe", (lambda e: e.matmul(PS[bk][:], ones[:], sq4[:, c, :], start=(c == 0), stop=(c == 3))),
                         reads=[sq4B, cB], writes=[PSb[bk]])
                rs, rsb = rstd_from(bk, 1.0 / 512, 512)
                for c in range(4):
                    gi = 48 + g * 4 + c
                    P.op("dve", (lambda e: e.scalar_tensor_tensor(
                        out=xn[g][:, c, :], in0=lc[:, g * 4 + c, :], scalar=gains[:, gi:gi + 1],
                        in1=rs, op0=ALU.mult, op1=ALU.mult)), reads=[lb, rsb, gB], writes=[xnB[g]])

        def mla_proj(tc):
            sl = slice(tc * 512, (tc + 1) * 512)
            xn, xnB = xn2[tc % 2], xnB2[tc % 2]
            for h in range(4):
                bk = gbank()
                for c in range(4):
                    P.op("pe", (lambda e: e.matmul(PS[bk][:], wuq[:, c, h * 192:h * 192 + 128], xn[0][:, c, :],
                                                   start=(c == 0), stop=(c == 3))),
                         reads=[mwB, xnB[0]], writes=[PSb[bk]])
                store_bf(qT[h], qTB[h], SC192)(tc, sl, bk)
                b1, b2 = gbank(), gbank()
                for (bk_, w_, c0_) in ((b1, wuq, h * 192 + 128), (b2, wuqr, h * 64)):
                    for c in range(4):
                        P.op("pe", (lambda e: e.matmul(PS[bk_][0:64, :], w_[:, c, c0_:c0_ + 64], xn[0][:, c, :],
                                                       start=(c == 0), stop=(c == 3))),
                             reads=[mwB, xnB[0]], writes=[PSb[bk_]])
                rope_evac(b1, b2, 64, sl, c_cosM, c_sinM, qrT[h], qrTB, SC192)
            for h in range(4):
                bk = gbank()
                for c in range(4):
                    P.op("pe", (lambda e: e.matmul(PS[bk][:], wukv[:, c, h * 256:h * 256 + 128], xn[1][:, c, :],
                                                   start=(c == 0), stop=(c == 3))),
                         reads=[mwB, xnB[1]], writes=[PSb[bk]])
                store_bf(kT[h], kTB[h])(tc, sl, bk)
            for t in range(4):
                tt = tc * 4 + t
                bk = gbank()
                for c in range(4):
                    P.op("pe", (lambda e: e.matmul(PS[bk][:], xn[1][:, c, t * 128:(t + 1) * 128], wukvv[:, c, :],
                                                   start=(c == 0), stop=(c == 3))),
                         reads=[mwB, xnB[1]], writes=[PSb[bk]])
                o, ob = ev_pool.get()
                P.op("dve", (lambda e: e.tensor_copy(out=o, in_=PS[bk][:])), reads=[PSb[bk]], writes=[ob])
                dma(vS[0:4, :, tt * 128:(tt + 1) * 128].rearrange("h p d -> p h d"),
                    o.rearrange("p (h d) -> p h d", h=4), reads=[ob], writes=vSB[0:4])


        mla_prep(0)
        for tc in range(NQ):
            if tc + 1 < NQ:
                mla_prep(tc + 1)
            mla_proj(tc)

        def slotv(s, i):
            return a0(s * 4 * S + i * S, S)
        slotB = [[Buf("sl%d_%d" % (s, i)) for i in range(4)] for s in range(2)]
        obuf = [f32v(a0(8 * S + h * 2 * S, 2 * S)) for h in range(4)]
        obufB = [Buf("ob%d" % h) for h in range(4)]
        a0_switch([b for s_ in slotB for b in s_] + obufB)
        kpe = A1[0:64, 0:S]
        kpeB = stageB
        dma(kpe, kpeT, reads=[kpeTB], writes=[kpeB])
        blk_ctr = [0]
        drot = [0]
        for m_ in range(4):
            NST = 6 if m_ == 3 else 4

            def head_views(h):
                s = (4 * m_ + h) % 2
                return tuple(slotv(s, i) for i in range(4)), slotB[s]

            def load_head(h):
                hh = 4 * m_ + h
                (QT, KTt, Vt, EX), (qb, kb, vb, eb) = head_views(h)
                dma(QT, qT[hh], reads=[qTB[hh]], writes=[qb])
                dma(KTt, kT[hh], reads=[kTB[hh]], writes=[kb])
                dma(Vt, vS[hh], reads=[vSB[hh]], writes=[vb])
                if m_ == 0:
                    dma(EX[0:64, :], qrT[h], reads=[qrTB], writes=[eb])
                if m_ == 2:
                    dma(EX[0:1, :], cqS[h:h + 1, :], reads=[cqSB], writes=[eb])

            def crange(j, kt):
                d0 = 512 * j - 128 * kt
                if m_ == 3:
                    return 0, 512
                c0 = max(0, -d0)
                c1 = 512
                if m_ == 1:
                    c1 = max(128, min(512, 128 + 2048 - d0))
                return c0, c1

            tiles = []
            head_start = {}
            for h in range(4):
                head_start[h] = len(tiles)
                for j in range(NQ):
                    if m_ == 1:
                        kts = [kt for kt in range(0, 4 * j + 4) if 512 * j - 128 * kt <= 2048]
                    else:
                        kts = list(range(0, 4 * j + 4))
                    if m_ == 3:
                        kts = kts[::-1]
                    else:
                        kts = sorted(kts, key=lambda kt: 0 if crange(j, kt) == (0, 512) else 1)
                        assert crange(j, kts[0]) == (0, 512)
                    bi = blk_ctr[0]
                    blk_ctr[0] += 1
                    for idx, kt in enumerate(kts):
                        tiles.append(dict(h=h, j=j, idx=idx, kt=kt, nk=len(kts), ob=4 + bi % 2, dn=6 + bi % 2))
            Rs = {}

            def st_scores(T):
                h, j, kt = T["h"], T["j"], T["kt"]
                (QT, KTt, Vt, EX), (qb, kb, vb, eb) = head_views(h)
                ks = slice(kt * 128, (kt + 1) * 128)
                d0 = 512 * j - 128 * kt
                c0, c1 = crange(j, kt)
                qa = slice(j * 512 + c0, j * 512 + c1)
                if m_ == 3:
                    sb_ = (0, 1, 2, 3, 6, 7)[drot[0] % 6]
                    drot[0] += 1
                else:
                    sb_ = gbank(4, 0)
                mm = [(KTt[:, ks], QT[:, qa], [kb, qb])]
                if m_ == 0:
                    mm.append((kpe[:, ks], EX[0:64, qa], [kpeB, eb]))
                if m_ == 2:
                    mm.append((ones[0:1, :], EX[0:1, qa], [cB, eb]))
                if m_ == 1:
                    mm.append((ident[:], maskB[:, d0 + 384 + c0:d0 + 384 + c1], [cB]))
                elif d0 < 128:
                    mk = maskD if m_ == 3 else maskA
                    mm.append((ident[:], mk[:, d0 + 384 + c0:d0 + 384 + c1], [cB]))
                n_ = len(mm)
                for i_, (lt, rh, rb_) in enumerate(mm):
                    P.op("pe", (lambda e: e.matmul(PS[sb_][:, c0:c1], lt, rh, start=(i_ == 0),
                                                   stop=(m_ != 3 and i_ == n_ - 1), skip_group_check=(m_ == 3))),
                         reads=rb_, writes=[PSb[sb_]])
                T["sb"] = sb_
                T["c"] = (c0, c1)

            def finalize(T):
                h, j = T["h"], T["j"]
                qs = slice(j * 512, (j + 1) * 512)
                ob_, dn_ = T["ob"], T["dn"]
                if m_ != 3:
                    rd, rdb = tm_pool.get()
                    P.op("dve", (lambda e: e.reciprocal(out=rd, in_=PS[dn_][:])), reads=[PSb[dn_]], writes=[rdb])
                    P.op("dve", (lambda e: e.tensor_tensor(out=obuf[h][:, qs], in0=PS[ob_][:], in1=rd, op=ALU.mult)),
                         reads=[PSb[ob_], rdb], writes=[obufB[h]])
                else:
                    P.op("dve", (lambda e: e.tensor_copy(out=obuf[h][:, qs], in_=PS[ob_][:])),
                         reads=[PSb[ob_]], writes=[obufB[h]])

            def st_exp(T):
                sb_ = T["sb"]
                c0, c1 = T["c"]
                pt, ptb = pt_pool.get()
                if m_ == 2:
                    ci = T["h"] * NT + T["kt"]
                    P.op("act", (lambda e: e.activation(out=pt[:, c0:c1], in_=PS[sb_][:, c0:c1], func=AF.Exp,
                                                        bias=cnegcol[:, ci:ci + 1])),
                         reads=[PSb[sb_], cnB], writes=[ptb])
                else:
                    P.op("act", (lambda e: e.activation(out=pt[:, c0:c1], in_=PS[sb_][:, c0:c1], func=AF.Exp)),
                         reads=[PSb[sb_]], writes=[ptb])
                T["pt"] = (pt, ptb)

            def st_pv(T):
                (QT, KTt, Vt, EX), (qb, kb, vb, eb) = head_views(T["h"])
                V3 = Vt.rearrange("p (t d) -> p t d", d=128)
                pt, ptb = T["pt"]
                c0, c1 = T["c"]
                kt = T["kt"]
                first, last = (T["idx"] == 0), (T["idx"] == T["nk"] - 1)
                ob_, dn_ = T["ob"], T["dn"]
                P.op("pe", (lambda e: e.matmul(PS[ob_][:, c0:c1], V3[:, kt, :], pt[:, c0:c1], start=first, stop=last,
                                               skip_group_check=True)), reads=[vb, ptb], writes=[PSb[ob_]])
                if m_ != 3:
                    P.op("pe", (lambda e: e.matmul(PS[dn_][:, c0:c1], ones[:], pt[:, c0:c1], start=first, stop=last,
                                                   skip_group_check=True)), reads=[cB, ptb], writes=[PSb[dn_]])
                if last:
                    finalize(T)

            def st_E(T):
                sb_ = T["sb"]
                ee, eeb = xf_pool.get()
                P.op("act", (lambda e: e.activation(out=ee, in_=PS[sb_][:], func=AF.Exp)),
                     reads=[PSb[sb_]], writes=[eeb])
                T["ee"] = (ee, eeb)

            def st_L(T):
                ee, eeb = T["ee"]
                ln_, lnb = ln_pool.get()
                P.op("act", (lambda e: e.activation(out=ln_, in_=ee, func=AF.Ln, bias=1.0)), reads=[eeb], writes=[lnb])
                T["ln"] = (ln_, lnb)
                key = (T["h"], T["j"], T["idx"])
                if T["idx"] < T["nk"] - 1:
                    Rn, Rnb = rr_pool.get()
                    if T["idx"] == 0:
                        P.op("dve", (lambda e: e.tensor_copy(out=Rn, in_=ln_)), reads=[lnb], writes=[Rnb])
                    else:
                        Rp, Rpb = Rs[(T["h"], T["j"], T["idx"] - 1)]
                        P.op("dve", (lambda e: e.tensor_tensor(out=Rn, in0=Rp, in1=ln_, op=ALU.add)),
                             reads=[lnb, Rpb], writes=[Rnb])
                    Rs[key] = (Rn, Rnb)

            def st_tri(T):
                sb_ = T["sb"]
                ln_, lnb = T["ln"]
                first = T["idx"] == 0
                P.op("pe", (lambda e: e.matmul(PS[sb_][:], negtri[:], ln_, start=False, stop=first,
                                               skip_group_check=True)), reads=[cB, lnb], writes=[PSb[sb_]])
                if not first:
                    Rp, Rpb = Rs[(T["h"], T["j"], T["idx"] - 1)]
                    P.op("pe", (lambda e: e.matmul(PS[sb_][:], negones[:], Rp, start=False, stop=True,
                                                   skip_group_check=True)), reads=[cB, Rpb], writes=[PSb[sb_]])

            def st_A(T):
                sb_ = T["sb"]
                pt, ptb = pt_pool.get()
                P.op("act", (lambda e: e.activation(out=pt, in_=PS[sb_][:], func=AF.Exp)), reads=[PSb[sb_]], writes=[ptb])
                T["pt"] = (pt, ptb)

            stages = ([st_scores, st_E, st_L, st_tri, st_A, st_pv] if m_ == 3
                      else [st_scores, (lambda T: None), st_exp, st_pv])
            load_head(0)
            load_head(1)
            load_at = {head_start[h - 1] + NST - 1: h for h in (2, 3)}
            for t in range(len(tiles) + NST - 1):
                if t in load_at:
                    load_head(load_at[t])
                for st in reversed(range(NST)):
                    ti = t - st
                    if 0 <= ti < len(tiles):
                        stages[st](tiles[ti])
            for tc in range(NQ):
                sl = slice(tc * 512, (tc + 1) * 512)
                bk = gbank(4, 0)
                for h in range(4):
                    s4, s4b = ln_pool.get()
                    P.op("act", (lambda e: e.activation(out=s4, in_=obuf[h][:, sl], func=AF.Square)),
                         reads=[obufB[h]], writes=[s4b])
                    P.op("pe", (lambda e: e.matmul(PS[bk][:], ones[:], s4, start=(h == 0), stop=(h == 3))),
                         reads=[s4b, cB], writes=[PSb[bk]])
                rs, rsb = rstd_from(bk, 1.0 / 512, 512)
                for h in range(4):
                    o, ob = ev_pool.get()
                    gi = 32 + (4 * m_ + h)
                    P.op("dve", (lambda e: e.scalar_tensor_tensor(out=o, in0=obuf[h][:, sl], scalar=gains[:, gi:gi + 1],
                                                                  in1=rs, op0=ALU.mult, op1=ALU.mult)),
                         reads=[obufB[h], rsb, gB], writes=[ob])
                    dma(attnT[4 * m_ + h][:, sl], o, reads=[ob], writes=[attnTB[4 * m_ + h]])

        a0_switch(hTb)
        attv = attnT.rearrange("c p s -> p c s")
        for tc in range(NQ):
            sl = slice(tc * 512, (tc + 1) * 512)
            dma(hT[:, :, sl], attv[:, :, sl], reads=attnTB, writes=hb(tc))
        wo = w_out[l].rearrange("(c p) n -> p c n", p=128)
        steps = []
        for t in range(8):
            def ld(t=t):
                wv, wb, _ = wtile(wo[:, :, t * 256:(t + 1) * 256], 16, 256)
                return wv, wb

            def cp(hd, t=t):
                for b in range(2):
                    gemm_fm(hd[0], hd[1], 16, b * 128, 128, hT, hb, resid_evac(t * 2 + b, xsrc))
            steps.append((ld, cp))
        pipelined(steps)

        norm_to_hT(yT, 16)

        wg = w_gate[l].rearrange("(c p) n -> p c n", p=128)
        wu = w_up[l].rearrange("(c p) n -> p c n", p=128)
        steps = []
        for fb in range(44):
            def ld(fb=fb):
                return wtile2(wg[:, :, fb * 128:(fb + 1) * 128], wu[:, :, fb * 128:(fb + 1) * 128], 16, 128)

            def cp(hd, fb=fb):
                wv, wb = hd
                for tc0 in range(0, NQ, 2):
                  tcs = [t_ for t_ in (tc0, tc0 + 1) if t_ < NQ]
                  ch_ = [(tc, gbank(8, 0), gbank(8, 0)) for tc in tcs]
                  for c in range(16):
                      for (tc, bg, bu) in ch_:
                          sl = slice(tc * 512, (tc + 1) * 512)
                          for (bk, c0_) in ((bg, 0), (bu, 128)):
                              P.op("pe", (lambda e: e.matmul(PS[bk][:], wv[:, c, c0_:c0_ + 128], hT[:, c, sl],
                                                             start=(c == 0), stop=(c == 15))),
                                   reads=[wb] + hb(tc), writes=[PSb[bk]])
                  for (tc, bg, bu) in ch_:
                    sl = slice(tc * 512, (tc + 1) * 512)
                    sg, sgb = xf_pool.get()
                    P.op("act", (lambda e: e.activation(out=sg, in_=PS[bg][:], func=AF.Silu)),
                         reads=[PSb[bg]], writes=[sgb])
                    o, ob = ev_pool.get()
                    P.op("dve", (lambda e: e.tensor_tensor(out=o, in0=PS[bu][:], in1=sg, op=ALU.mult)),
                         reads=[PSb[bu], sgb], writes=[ob])
                    dma(actT[fb][:, sl], o, reads=[ob], writes=[actTB[fb]])
            steps.append((ld, cp))
        pipelined(steps)

        PAN = min(1024, S)
        npan = S // PAN
        nch = PAN // 512
        aP = a0(0, 44 * PAN).rearrange("p (c s) -> p c s", c=44)
        aPB = [Buf("aP%d" % i) for i in range(4)]
        stage2B = Buf("stage2")
        a0_switch(aPB + [stage2B])
        stage_list.append((f32v(a0(44 * PAN, 11264)), stage2B))
        xres = [f32v(a0(44 * PAN + 11264 + i * 1024, 1024)) for i in range(8)]
        xresB = [Buf("xres%d" % i) for i in range(8)]
        alias(xresB, [stage2B])
        a0_cur[0] = a0_cur[0] + xresB
        xpre = {}
        actv = actT.rearrange("c p s -> p c s")
        wd = w_down[l].rearrange("(c p) n -> p c n", p=128)
        steps = []
        for pn in range(npan):
            ps_ = slice(pn * PAN, (pn + 1) * PAN)
            for cg in range(4):
                for kg in range(4):
                    def ld(pn=pn, cg=cg, kg=kg, ps_=ps_):
                        if cg == 0 and pn == 0:
                            dma(aP[:, kg * 11:(kg + 1) * 11, :], actv[:, kg * 11:(kg + 1) * 11, ps_],
                                reads=actTB[kg * 11:(kg + 1) * 11], writes=[aPB[kg]])
                        wv, wb, _ = wtile(wd[:, kg * 11:(kg + 1) * 11, cg * 512:(cg + 1) * 512], 11, 512)
                        return wv, wb

                    def cp(hd, pn=pn, cg=cg, kg=kg):
                        wv, wb = hd
                        if kg == 2:
                            for fbi in range(4):
                                for ch in range(nch):
                                    bk = fbi * nch + ch
                                    sl = slice(pn * PAN + ch * 512, pn * PAN + (ch + 1) * 512)
                                    xpre[bk] = resid_load(cg * 4 + fbi, yT, sl, (xres[bk], xresB[bk]))
                        for k in range(11):
                            kk_ = kg * 11 + k
                            for fbi in range(4):
                                for ch in range(nch):
                                    bk = fbi * nch + ch
                                    P.op("pe", (lambda e: e.matmul(PS[bk][:], wv[:, k, fbi * 128:(fbi + 1) * 128],
                                                                   aP[:, kk_, ch * 512:(ch + 1) * 512],
                                                                   start=(kk_ == 0), stop=(kk_ == 43))),
                                         reads=[wb, aPB[kg]], writes=[PSb[bk]])
                        if cg == 3 and pn + 1 < npan:
                            pn1 = slice((pn + 1) * PAN, (pn + 2) * PAN)
                            dma(aP[:, kg * 11:(kg + 1) * 11, :], actv[:, kg * 11:(kg + 1) * 11, pn1],
                                reads=actTB[kg * 11:(kg + 1) * 11], writes=[aPB[kg]])
                        if kg == 3:
                            for fbi in range(4):
                                for ch in range(nch):
                                    bk = fbi * nch + ch
                                    fb = cg * 4 + fbi
                                    sl = slice(pn * PAN + ch * 512, pn * PAN + (ch + 1) * 512)
                                    xo, xob = xpre.pop(bk)
                                    resid_add_store(fb, sl, bk, xo, xob)
                    steps.append((ld, cp))
        pipelined(steps)
        stage_list.pop()

    def fin_dst(i, sl, rs, rsb, xc, xcB):
        for c in range(16):
            o, ob = xf_pool.get()
            P.op("dve", (lambda e: e.scalar_tensor_tensor(
                out=o[:, 0:256], in0=xc[:, c, :], scalar=gains[:, 56 + c:57 + c], in1=rs[:, 0:256],
                op0=ALU.mult, op1=ALU.mult)), reads=[xcB, rsb, gB], writes=[ob])
            dma(yT[c * 128:(c + 1) * 128, sl], o[:, 0:256], reads=[ob], writes=[yTB[c]])
    norm_phase(yT, fin_dst)

    P.finish()
    with nc.allow_non_contiguous_dma(reason="small strided loads"):
        P.emit()
    es.close()
    return nc


WNAMES = ["attn_norm", "w_in", "mla_q_norm", "w_uq", "mla_kv_norm", "w_ukv", "fox_forget_bias", "group_norm",
          "w_out", "ffn_norm", "w_gate", "w_up", "w_down", "final_norm"]


def run(inputs, S, DEPTH, ncores):
    nc = build(S, DEPTH)
    consts = host_consts(S)
    x = np.asarray(inputs["x"], dtype=np.float32)
    shared = {k: np.ascontiguousarray(np.asarray(inputs[k], dtype=np.float32)) for k in WNAMES}
    shared.update(consts)
    in_maps = []
    for b in range(ncores):
        m = dict(shared)
        m["xT"] = np.ascontiguousarray(x[b].T)
        in_maps.append(m)
    res = run_bass_kernel_spmd(nc, in_maps, core_ids=list(range(ncores)))
    out = np.stack([np.ascontiguousarray(r["yT"].T) for r in res.results], axis=0)
    return out.astype(np.float32)


def kernel(**inputs):
    x = inputs["x"]
    return run(inputs, x.shape[1], inputs["w_in"].shape[0], x.shape[0])
```
